# Optimizing a Trainium2 kernel written in Bass

```python
import jax, jax.numpy as jnp
from jax import lax
import numpy as np

D_MODEL = 2048
BATCH = 2
SEQ = 16384
DEPTH = 1

D_MIX = D_MODEL
HEAD_DIM = 64
N_Q_HEADS = 16
N_KV_HEADS = 4
Q_PER_KV = N_Q_HEADS // N_KV_HEADS
ATTN_WIDTH = N_Q_HEADS * HEAD_DIM
KV_WIDTH = N_KV_HEADS * HEAD_DIM
WINDOW = 128
BLOCK = 128
ROT_DIM = HEAD_DIM // 4
ROPE_THETA = 500000.0
CHUNK = 128
GMLP_GROUP_DIM = 128
GMLP_WIDTH = D_MIX - ATTN_WIDTH
N_GMLP_GROUPS = GMLP_WIDTH // GMLP_GROUP_DIM
IN_COLS = ATTN_WIDTH + 2 * KV_WIDTH + 2 * GMLP_WIDTH
D_FF = 5632
EPS = 1e-6
NEG_INF = -1e30

kernel_name = "hymba_swa_sink_gmlp_macaron"


def rmsnorm(x, g):
    xf = x.astype(jnp.float32)
    y = xf * lax.rsqrt(jnp.mean(xf * xf, axis=-1, keepdims=True) + EPS)
    return (y * g.astype(jnp.float32)).astype(x.dtype)


def swiglu(h, w_gate, w_up, w_down):
    return (jax.nn.silu(h @ w_gate) * (h @ w_up)) @ w_down


def rope_tables(positions, dtype):
    inv_freq = ROPE_THETA ** (-jnp.arange(0, ROT_DIM, 2, dtype=jnp.float32) / ROT_DIM)
    ang = positions.astype(jnp.float32)[..., None] * inv_freq
    return jnp.cos(ang)[:, :, None, :].astype(dtype), jnp.sin(ang)[:, :, None, :].astype(dtype)


def partial_rope(t, cos, sin):
    half = ROT_DIM // 2
    t1, t2, rest = t[..., :half], t[..., half:ROT_DIM], t[..., ROT_DIM:]
    return jnp.concatenate([t1 * cos - t2 * sin, t2 * cos + t1 * sin, rest], axis=-1)


def band(t):
    b, s, h, d = t.shape
    tb = t.reshape(b, s // BLOCK, BLOCK, h, d)
    prev = jnp.pad(tb, ((0, 0), (1, 0), (0, 0), (0, 0), (0, 0)))[:, :-1]
    return jnp.concatenate([prev, tb], axis=2)


def sliding_window_attention_with_sinks(q, k, v, sinks):
    b, s = q.shape[:2]
    nb = s // BLOCK
    qb = q.reshape(b, nb, BLOCK, N_KV_HEADS, Q_PER_KV, HEAD_DIM)
    kb, vb = band(k), band(v)
    scores = jnp.einsum('bnqhgd,bnkhd->bnhgqk', qb, kb,
                        preferred_element_type=jnp.float32) * (HEAD_DIM ** -0.5)
    qi = jnp.arange(BLOCK)[:, None]
    kj = jnp.arange(2 * BLOCK)[None, :]
    diff = qi + BLOCK - kj
    blk = jnp.arange(nb)[:, None, None]
    valid = (diff >= 0) & (diff < WINDOW) & (blk * BLOCK + kj - BLOCK >= 0)
    scores = jnp.where(valid[None, :, None, None], scores, NEG_INF)
    sink = sinks.astype(jnp.float32).reshape(N_KV_HEADS, Q_PER_KV)[None, None, :, :, None, None]
    m = jnp.maximum(jnp.max(scores, axis=-1, keepdims=True), sink)
    p = jnp.exp(scores - m)
    p = p / (jnp.sum(p, axis=-1, keepdims=True) + jnp.exp(sink - m))
    out = jnp.einsum('bnhgqk,bnkhd->bnqhgd', p.astype(v.dtype), vb)
    return out.reshape(b, s, ATTN_WIDTH)


def chunked_spatial_gating(u, v, w_s, b_s):
    b, s = u.shape[:2]
    nc = s // CHUNK
    v4 = v.reshape(b, nc, CHUNK, N_GMLP_GROUPS, GMLP_GROUP_DIM)
    causal = jnp.tril(jnp.ones((CHUNK, CHUNK), dtype=bool))
    w = jnp.where(causal[None], w_s, jnp.zeros_like(w_s))
    sp = jnp.einsum('gts,bnsgc->bntgc', w, v4) + b_s.T[None, None, :, :, None]
    return u * sp.reshape(b, s, GMLP_WIDTH)


def setup_inputs(seed: int = 0) -> dict:
    key = jax.random.key(seed)
    ks = jax.random.split(key, 24)
    f32 = jnp.float32

    def nrm(k, shape, scale):
        return jax.random.normal(k, shape, f32) * scale

    def gain(k, shape):
        return 1.0 + 0.1 * jax.random.normal(k, shape, f32)

    L = DEPTH
    x = jax.random.normal(ks[0], (BATCH, SEQ, D_MODEL), f32)
    positions = (jnp.arange(SEQ, dtype=jnp.int32)[None, :]
                 + jax.random.randint(ks[1], (BATCH, 1), 0, 4096, dtype=jnp.int32))
    return {
        "x": x,
        "positions": positions,
        "ffn1_norm": gain(ks[2], (L, D_MODEL)),
        "ffn1_w_gate": nrm(ks[3], (L, D_MODEL, D_FF), D_MODEL ** -0.5),
        "ffn1_w_up": nrm(ks[4], (L, D_MODEL, D_FF), D_MODEL ** -0.5),
        "ffn1_w_down": nrm(ks[5], (L, D_FF, D_MODEL), D_FF ** -0.5),
        "mix_norm": gain(ks[6], (L, D_MODEL)),
        "w_in": nrm(ks[7], (L, D_MODEL, IN_COLS), D_MODEL ** -0.5),
        "q_norm": gain(ks[8], (L, HEAD_DIM)),
        "k_norm": gain(ks[9], (L, HEAD_DIM)),
        "attn_sinks": nrm(ks[10], (L, N_Q_HEADS), 1.0),
        "gmlp_v_norm": gain(ks[11], (L, GMLP_WIDTH)),
        "gmlp_w_s": nrm(ks[12], (L, N_GMLP_GROUPS, CHUNK, CHUNK), CHUNK ** -0.5),
        "gmlp_b_s": 1.0 + 0.1 * jax.random.normal(ks[13], (L, N_GMLP_GROUPS, CHUNK), f32),
        "attn_out_norm": gain(ks[14], (L, ATTN_WIDTH)),
        "gmlp_out_norm": gain(ks[15], (L, GMLP_WIDTH)),
        "w_out": nrm(ks[16], (L, D_MIX, D_MODEL), D_MIX ** -0.5),
        "ffn2_norm": gain(ks[17], (L, D_MODEL)),
        "ffn2_w_gate": nrm(ks[18], (L, D_MODEL, D_FF), D_MODEL ** -0.5),
        "ffn2_w_up": nrm(ks[19], (L, D_MODEL, D_FF), D_MODEL ** -0.5),
        "ffn2_w_down": nrm(ks[20], (L, D_FF, D_MODEL), D_FF ** -0.5),
    }


def reference(x, positions, ffn1_norm, ffn1_w_gate, ffn1_w_up, ffn1_w_down, mix_norm, w_in,
              q_norm, k_norm, attn_sinks, gmlp_v_norm, gmlp_w_s, gmlp_b_s, attn_out_norm,
              gmlp_out_norm, w_out, ffn2_norm, ffn2_w_gate, ffn2_w_up, ffn2_w_down):
    b, s, _ = x.shape
    cos, sin = rope_tables(positions, x.dtype)
    splits = np.cumsum([ATTN_WIDTH, KV_WIDTH, KV_WIDTH, GMLP_WIDTH]).tolist()
    for l in range(DEPTH):
        x = x + 0.5 * swiglu(rmsnorm(x, ffn1_norm[l]), ffn1_w_gate[l], ffn1_w_up[l], ffn1_w_down[l])

        h = rmsnorm(x, mix_norm[l])
        z = h @ w_in[l]
        q, k, v, gu, gv = jnp.split(z, splits, axis=-1)

        q = rmsnorm(q.reshape(b, s, N_Q_HEADS, HEAD_DIM), q_norm[l])
        k = rmsnorm(k.reshape(b, s, N_KV_HEADS, HEAD_DIM), k_norm[l])
        v = v.reshape(b, s, N_KV_HEADS, HEAD_DIM)
        q = partial_rope(q, cos, sin)
        k = partial_rope(k, cos, sin)
        a_out = sliding_window_attention_with_sinks(q, k, v, attn_sinks[l])

        gu = jax.nn.gelu(gu)
        gv = jax.nn.gelu(gv).reshape(b, s, N_GMLP_GROUPS, GMLP_GROUP_DIM)
        gv = rmsnorm(gv, gmlp_v_norm[l].reshape(N_GMLP_GROUPS, GMLP_GROUP_DIM)).reshape(b, s, GMLP_WIDTH)
        g_out = chunked_spatial_gating(gu, gv, gmlp_w_s[l], gmlp_b_s[l])

        mixed = jnp.concatenate([rmsnorm(a_out, attn_out_norm[l]),
                                 rmsnorm(g_out, gmlp_out_norm[l])], axis=-1)
        x = x + mixed @ w_out[l]

        x = x + 0.5 * swiglu(rmsnorm(x, ffn2_norm[l]), ffn2_w_gate[l], ffn2_w_up[l], ffn2_w_down[l])
    return x
```

```python
import math
from contextlib import ExitStack

import numpy as np
import concourse.bass as bass
import concourse.mybir as mybir
from concourse.bass_utils import run_bass_kernel_spmd

F32 = mybir.dt.float32
BF16 = mybir.dt.bfloat16
I32 = mybir.dt.int32
AF = mybir.ActivationFunctionType
ALU = mybir.AluOpType
AX = mybir.AxisListType

D = 2048
DFF = 5632
NCH = 16
NF = 44
T = 512
NB = T // 128
NCORES = 8
SEQ = 16384
TOK_PER_CORE = 4096
HALO = 128
IN_COLS = 3584
EPS = 1e-6
NSLOT = 4
COMPUTE = ("pe", "act", "dve", "pool")


class Cell:
    __slots__ = ("w", "r", "rd", "excl")

    def __init__(self, excl=False):
        self.w = None
        self.r = {}
        self.rd = []
        self.excl = excl


class Op:
    __slots__ = ("eng", "fn", "deps", "idx", "signal", "is_dma", "sem", "semval")

    def __init__(self, eng, fn, is_dma):
        self.eng = eng
        self.fn = fn
        self.deps = []
        self.signal = False
        self.is_dma = is_dma
        self.sem = None
        self.semval = 0
        self.idx = 0


class DSem:
    def __init__(self, name):
        self.name = name
        self.handle = None
        self.count = 0


class Prog:
    def __init__(self):
        self.ops = {e: [] for e in ("pe", "act", "dve", "pool", "sp")}
        self.dsems = []
        self.final_ops = []

    def dsem(self, name):
        s = DSem(name)
        self.dsems.append(s)
        return s

    def emit(self, eng, fn, reads=(), writes=(), dsem=None):
        is_dma = dsem is not None
        op = Op(eng, fn, is_dma)
        lst = self.ops[eng]
        op.idx = len(lst)
        deps = {}
        if any(c.excl for c in reads):
            reads = list(reads)
            writes = list(writes) + [c for c in reads if c.excl and c not in writes]
            reads = [c for c in reads if not c.excl]
        for c in reads:
            if c.w is not None:
                deps[id(c.w)] = c.w
        for c in writes:
            if c.w is not None:
                deps[id(c.w)] = c.w
            for o in c.r.values():
                deps[id(o)] = o
            for o in c.rd:
                deps[id(o)] = o
        for c in reads:
            if is_dma:
                c.rd.append(op)
            else:
                c.r[eng] = op
        for c in writes:
            c.w = op
            c.r = {}
            c.rd = []
        op.deps = list(deps.values())
        if is_dma:
            op.sem = dsem
            dsem.count += 16
            op.semval = dsem.count
        lst.append(op)
        return op

    def finalize(self):
        for e, lst in self.ops.items():
            for op in lst:
                nd = []
                for d in op.deps:
                    if (not d.is_dma) and (not op.is_dma) and d.eng == "pe" and e == "pe":
                        continue
                    nd.append(d)
                    d.signal = True
                op.deps = nd
        for op in self.final_ops:
            op.signal = True
        for e in COMPUTE:
            n = 0
            for op in self.ops[e]:
                if op.is_dma:
                    continue
                if op.signal:
                    n += 1
                    op.semval = n


def run_prog(nc, prog, st):
    prog.finalize()
    csem = {e: st.enter_context(nc.semaphore("c_" + e)) for e in COMPUTE}
    for s in prog.dsems:
        s.handle = st.enter_context(nc.semaphore("d_" + s.name))
    block = st.enter_context(nc.Block())

    def keyof(d):
        if d.is_dma:
            return ("d", id(d.sem)), d.sem.handle
        return ("c", d.eng), csem[d.eng]

    def replay(ename, eng):
        seen = {}
        for op in prog.ops[ename]:
            for d in op.deps:
                key, h = keyof(d)
                if seen.get(key, 0) >= d.semval:
                    continue
                seen[key] = d.semval
                eng.wait_ge(h, d.semval)
            ins = op.fn(eng)
            if op.is_dma:
                ins.then_inc(op.sem.handle, 16)
            elif op.signal:
                ins.then_inc(csem[op.eng], 1)
        if ename == "sp":
            for d in prog.final_ops:
                key, h = keyof(d)
                if seen.get(key, 0) >= d.semval:
                    continue
                seen[key] = d.semval
                eng.wait_ge(h, d.semval)

    @block.tensor
    def _(eng):
        replay("pe", eng)

    @block.scalar
    def _(eng):
        replay("act", eng)

    @block.vector
    def _(eng):
        replay("dve", eng)

    @block.gpsimd
    def _(eng):
        replay("pool", eng)

    @block.sync
    def _(eng):
        replay("sp", eng)


def build_program(npass=8, do_mixer=True, do_ffn2=True, do_ffn1=True, mix_level=4):
    nc = bass.Bass("TRN2", target_bir_lowering=False)
    P = Prog()
    st = ExitStack()

    def din(name, shape, dt=F32):
        return nc.dram_tensor(name, list(shape), dt, kind="ExternalInput").ap()

    ntok_in = HALO + TOK_PER_CORE
    x_d = din("x", [ntok_in, D])
    pos_d = din("pos", [1, ntok_in], I32)
    y_d = nc.dram_tensor("y", [TOK_PER_CORE, D], F32, kind="ExternalOutput").ap()
    wsrc = {
        "g1": din("w_g1", [D, DFF]), "u1": din("w_u1", [D, DFF]), "d1": din("w_d1", [DFF, D]),
        "in": din("w_in", [D, IN_COLS]), "out": din("w_out", [D, D]),
        "g2": din("w_g2", [D, DFF]), "u2": din("w_u2", [D, DFF]), "d2": din("w_d2", [DFF, D]),
    }
    wbf = {k: nc.dram_tensor("wb_" + k, list(v.shape), BF16, kind="Internal").ap() for k, v in wsrc.items()}
    gains_d = din("gains", [128, 48])
    gqk_d = din("gqk", [128, 2])
    g8_d = din("g8", [128, 24])
    sinks_d = din("sinks", [1, 16])
    wsT_d = din("wsT", [128, 8 * 128])
    bs_d = din("bs", [1, 8 * 128])
    ident_d = din("ident", [128, 128])
    bd_d = din("bd", [128, 128])
    perm_d = din("perm", [128, 128])
    mask_d = din("mask", [128, 512])
    mask0_d = din("mask0", [128, 512])
    freq_d = din("freq", [128, 2])

    DBG = mix_level >= 100
    if DBG:
        dbg_f = nc.dram_tensor("dbg_f", [128, 4096], F32, kind="ExternalOutput").ap()
        dbg_b = nc.dram_tensor("dbg_b", [128, 4096], BF16, kind="ExternalOutput").ap()
    dbg_ops = []

    def dump(kind, col0, src, cells_):
        if not DBG or dry[0]:
            return
        n = src.shape[-1]
        dst = (dbg_f if kind == "f" else dbg_b)[:, col0:col0 + n]
        dbg_ops.append(P.emit("sp", lambda e: e.dma_start(out=dst, in_=src), reads=cells_, dsem=s_dbg))

    def sb(name, shape, dt=F32):
        return st.enter_context(nc.sbuf_tensor("s_" + name, list(shape), dt))

    def cells(n):
        return [Cell() for _ in range(n)]

    xT = sb("xT", [128, NCH, T]); xT_c = cells(NCH)
    hT = sb("hT", [128, NCH, T], BF16); hT_c = cells(NCH)
    mixA = sb("mixA", [128, 8, T], BF16); mixA_c = cells(8)
    big = sb("big", [128, NF * T // 2])
    big_c = cells(NF)
    big_bf = big[:].bitcast(BF16)
    wsl = sb("wsl", [128, NSLOT, 8192], BF16); wsl_c = cells(NSLOT)
    kT = sb("kT", [128, 2, (NB + 1) * 128], BF16); kT_c = cells(NB + 1)
    Vaug = sb("Vaug", [128, NB + 1, 4, 65], BF16); V_c = cells(NB + 1)
    ident = sb("ident", [128, 128]); c_ident = Cell()
    ones_bf = sb("ones_bf", [128, 128], BF16); c_ones = Cell()
    bd_bf = sb("bd_bf", [128, 128], BF16); c_bd = Cell()
    perm_bf = sb("perm_bf", [128, 128], BF16); c_perm = Cell()
    mask_bf = sb("mask_bf", [128, 512], BF16); c_mask = Cell()
    mask0_bf = sb("mask0_bf", [128, 512], BF16); c_mask0 = Cell()
    wsT = sb("wsT", [128, 8, 128], BF16); c_wsT = Cell()
    bias_bc = sb("bias_bc", [128, 8, 128]); c_bias = Cell()
    gains = sb("gains", [128, 48]); c_gains = Cell()
    gqk = sb("gqk", [128, 2]); c_gqk = Cell()
    g8 = sb("g8", [128, 24]); c_g8 = Cell()
    esink = sb("esink", [128, 16]); c_esink = Cell()
    freq = sb("freq", [128, 2]); c_freq = Cell()
    cst = sb("cst", [128, 4]); c_cst = Cell()
    COS = sb("COS", [128, T]); c_cos = Cell()
    SIN = sb("SIN", [128, T]); c_sin = Cell()
    posi = sb("posi", [128, T], I32); c_posi = Cell()
    sq = [sb("sq%d" % i, [128, T], BF16) for i in range(4)]; sq_c = cells(4)
    rstd = sb("rstd", [128, T]); c_rstd = Cell()
    sdt = sb("sdt", [128, T]); c_sdt = Cell()
    silu_t = [sb("silu%d" % i, [128, T]) for i in range(2)]; silu_c = cells(2)
    tA = [sb("tA%d" % i, [128, T]) for i in range(2)]; tA_c = cells(2)
    tB = [sb("tB%d" % i, [128, T]) for i in range(2)]; tB_c = cells(2)
    qn = [sb("qn%d" % i, [128, T], BF16) for i in range(2)]; qn_c = cells(2)
    rt = [tA[0], tA[1], tB[0]]; rt_c = [tA_c[0], tA_c[1], tB_c[0]]
    ldtmp = tB[1]; c_ldtmp = tB_c[1]
    junk = sdt[:].bitcast(BF16); c_junk = c_sdt
    small = sb("small", [128, 64]); small_c = cells(8)

    ps = st.enter_context(nc.psum_tensor("ps", [128, 8 * 512], F32))
    ps_c = [Cell(excl=True) for _ in range(8)]
    bank_ctr = [0]

    held = set()

    def bank(hold=False):
        while True:
            b = bank_ctr[0] % 8
            bank_ctr[0] += 1
            if b not in held:
                break
        if hold:
            held.add(b)
        return ps[:, b * 512:(b + 1) * 512], ps_c[b]

    def release(pc_):
        held.discard(ps_c.index(pc_))

    rr = {"sq": 0, "silu": 0, "tset": 0, "pt": 0, "stg": 0, "small": 0}

    def nxt(k, n):
        v = rr[k] % n
        rr[k] += 1
        return v

    def hid(j):
        return big_bf[:, j * T:(j + 1) * T], [big_c[j]]

    def qT(m):
        return big_bf[:, m * T:(m + 1) * T], [big_c[m]]

    def gu(g):
        return big[:, 4 * T + g * T: 4 * T + (g + 1) * T], [big_c[8 + 2 * g], big_c[9 + 2 * g]]

    def gvn(b):
        return big_bf[:, 24 * T + b * 1024: 24 * T + (b + 1) * 1024], [big_c[24 + 2 * b], big_c[25 + 2 * b]]

    def aout(i):
        o = 16 * T + i * 1024
        return big[:, o:o + 1024], big_c[32 + 4 * i: 36 + 4 * i]

    def ptb(i):
        return big_bf[:, (40 + i) * T:(41 + i) * T], [big_c[40 + i]]

    def stg(i):
        o = (28 + 8 * i) * (T // 2)
        return big[:, o:o + 2048], big_c[28 + 8 * i: 36 + 8 * i]

    s_const = P.dsem("const")
    s_slot = [P.dsem("slot%d" % i) for i in range(NSLOT)]
    s_xin = [P.dsem("xin%d" % i) for i in range(2)]
    s_xout = [P.dsem("xout%d" % i) for i in range(2)]
    s_pos = P.dsem("pos")
    s_dbg = P.dsem("dbg")

    dry = [False]

    def E(eng, fn, reads=(), writes=(), dsem=None):
        if dry[0]:
            return None
        return P.emit(eng, fn, reads, writes, dsem)

    nconst = [0]

    def load_const(dst_ap, src_ap, cell):
        nconst[0] += 1
        E("sp", lambda e: e.dma_start(out=dst_ap, in_=src_ap), writes=[cell], dsem=P.dsem("const%d" % nconst[0]))

    load_const(ident[:], ident_d, c_ident)
    load_const(gains[:], gains_d, c_gains)
    load_const(gqk[:], gqk_d, c_gqk)
    load_const(g8[:], g8_d, c_g8)
    load_const(freq[:], freq_d, c_freq)
    load_const(esink[:], sinks_d.partition_broadcast(128), c_esink)
    load_const(bias_bc[:].rearrange("p g t -> p (g t)"), bs_d.partition_broadcast(128), c_bias)
    E("dve", lambda e: e.memset(ones_bf[:], 1.0), writes=[c_ones])
    E("dve", lambda e: e.memset(cst[:, 0:1], EPS), writes=[c_cst])
    E("dve", lambda e: e.memset(cst[:, 1:2], math.pi / 2), writes=[c_cst])
    E("dve", lambda e: e.memset(Vaug[:].rearrange("p b h d -> p (b h d)"), 1.0), writes=V_c)
    E("dve", lambda e: e.memset(kT[:].rearrange("p c t -> p (c t)"), 0.0), writes=kT_c)
    E("act", lambda e: e.activation(out=esink[:], in_=esink[:], func=AF.Exp), reads=[c_esink], writes=[c_esink])
    E("dve", lambda e: e.tensor_scalar(out=gqk[:, 0:1], in0=gqk[:, 0:1], scalar1=0.125, scalar2=None, op0=ALU.mult),
      reads=[c_gqk], writes=[c_gqk])
    for (src, dst, cdst, n) in ((bd_d, bd_bf[:], c_bd, 128), (perm_d, perm_bf[:], c_perm, 128),
                                (mask_d, mask_bf[:], c_mask, 512), (mask0_d, mask0_bf[:], c_mask0, 512)):
        load_const(ldtmp[:, 0:n], src, c_ldtmp)
        E("dve", lambda e, dst=dst, n=n: e.tensor_copy(out=dst, in_=ldtmp[:, 0:n]), reads=[c_ldtmp], writes=[cdst])
    for hf in range(2):
        load_const(ldtmp[:, 0:512], wsT_d[:, hf * 512:(hf + 1) * 512], c_ldtmp)
        E("dve", lambda e, hf=hf: e.tensor_tensor(out=wsT[:, hf * 4:(hf + 1) * 4, :], in0=ldtmp[:, 0:512].rearrange("p (g t) -> p g t", g=4),
                                                  in1=mask_bf[:, 128:256].unsqueeze(1).broadcast_to([128, 4, 128]), op=ALU.mult),
          reads=[c_ldtmp, c_mask], writes=[c_wsT])

    conv_cells = {}
    conv_order = []

    def conv(key, cg, kind, pc):
        half = (pc // 2) if kind == "D" else 0
        ck = (key, cg, half)
        if ck in conv_cells:
            return
        c = Cell()
        conv_cells[ck] = c
        s = P.dsem("cv_%s_%d_%d" % (key, cg, half))
        if kind == "D":
            r0, r1 = half * (DFF // 2), (half + 1) * (DFF // 2)
        else:
            r0, r1 = 0, D
        src = wsrc[key][r0:r1, cg * 512:(cg + 1) * 512]
        dst = wbf[key][r0:r1, cg * 512:(cg + 1) * 512]
        conv_order.append(c)
        thr = [conv_order[-5]] if len(conv_order) >= 5 else []
        E("pool", lambda e: e.dma_start(out=dst, in_=src), reads=thr, writes=[c], dsem=s)

    def ffn_tiles(g, u, d):
        tl = []
        for fg in range(11):
            tl.append((g, "A", fg, 0))
            tl.append((u, "A", fg, 0))
        for dg in range(4):
            for pc in range(4):
                tl.append((d, "D", dg, pc))
        return tl

    stream = []
    loaded = [0]
    consumed = [0]

    def load_tile(i):
        key, kind, cg, pc = stream[i]
        s = i % NSLOT
        if kind == "A":
            src = wbf[key][:, cg * 512:(cg + 1) * 512].rearrange("(c p) n -> p c n", p=128)
            dst = wsl[:, s, :].rearrange("p (c n) -> p c n", n=512)
        else:
            src = wbf[key][pc * 11 * 128:(pc + 1) * 11 * 128, cg * 512:(cg + 1) * 512].rearrange("(c p) n -> p c n", p=128)
            dst = wsl[:, s, 0:11 * 512].rearrange("p (c n) -> p c n", n=512)
        E("sp", lambda e: e.dma_start(out=dst, in_=src), reads=[conv_cells[(key, cg, (pc // 2) if kind == "D" else 0)]], writes=[wsl_c[s]], dsem=s_slot[s])

    def next_tile(expect):
        if dry[0]:
            stream.append(expect)
            return wsl[:, 0, :].rearrange("p (c n) -> p c n", n=512), wsl_c[0]
        i = consumed[0]
        assert stream[i] == expect, (stream[i], expect)
        while loaded[0] < min(len(stream), i + NSLOT - 1):
            load_tile(loaded[0])
            loaded[0] += 1
        consumed[0] += 1
        s = i % NSLOT
        return wsl[:, s, :].rearrange("p (c n) -> p c n", n=512), wsl_c[s]

    def load_x(row0, TT):
        for b in range(TT // 128):
            si = nxt("stg", 2)
            sap, sc = stg(si)
            r0 = row0 + b * 128
            E("sp", lambda e, sap=sap, r0=r0: e.dma_start(out=sap, in_=x_d[r0:r0 + 128, :]), writes=sc, dsem=s_xin[si])
            for cgp in range(4):
                pb, pc_ = bank()
                for k in range(4):
                    c = cgp * 4 + k
                    E("pe", lambda e, pb=pb, sap=sap, c=c, k=k: e.transpose(pb[:, k * 128:(k + 1) * 128], sap[:, c * 128:(c + 1) * 128], ident[:]),
                      reads=sc + [c_ident], writes=[pc_])
                E("dve" if cgp % 2 else "act",
                  (lambda e, pb=pb, cgp=cgp, b=b: e.tensor_copy(out=xT[:, cgp * 4:cgp * 4 + 4, b * 128:(b + 1) * 128], in_=pb.rearrange("p (k t) -> p k t", k=4)))
                  if cgp % 2 else
                  (lambda e, pb=pb, cgp=cgp, b=b: e.activation(out=xT[:, cgp * 4:cgp * 4 + 4, b * 128:(b + 1) * 128], in_=pb.rearrange("p (k t) -> p k t", k=4), func=AF.Copy)),
                  reads=[pc_], writes=xT_c[cgp * 4:cgp * 4 + 4])

    out_ops = []

    def store_x(row0, TT):
        for b in range(TT // 128):
            si = nxt("stg", 2)
            sap, sc = stg(si)
            for cgp in range(4):
                pb, pc_ = bank()
                for k in range(4):
                    c = cgp * 4 + k
                    E("pe", lambda e, pb=pb, c=c, k=k, b=b: e.transpose(pb[:, k * 128:(k + 1) * 128], xT[:, c, b * 128:(b + 1) * 128], ident[:]),
                      reads=[xT_c[c], c_ident], writes=[pc_])
                if cgp % 2:
                    E("dve", lambda e, pb=pb, sap=sap, cgp=cgp: e.tensor_copy(out=sap[:, cgp * 512:(cgp + 1) * 512], in_=pb), reads=[pc_], writes=sc)
                else:
                    E("act", lambda e, pb=pb, sap=sap, cgp=cgp: e.activation(out=sap[:, cgp * 512:(cgp + 1) * 512], in_=pb, func=AF.Copy), reads=[pc_], writes=sc)
            r0 = row0 + b * 128
            op = E("sp", lambda e, sap=sap, r0=r0: e.dma_start(out=y_d[r0:r0 + 128, :], in_=sap), reads=sc, dsem=s_xout[si])
            out_ops.append(op)

    def rsqrt_tile(src_ps, src_cell, scale, TT, dst, dst_cell):
        E("act", lambda e: e.activation(out=sdt[:, :TT], in_=src_ps[:, :TT], func=AF.Ln, bias=cst[:, 0:1], scale=scale),
          reads=[src_cell, c_cst], writes=[c_sdt])
        E("act", lambda e: e.activation(out=dst[:, :TT], in_=sdt[:, :TT], func=AF.Exp, scale=-0.5), reads=[c_sdt], writes=[dst_cell])

    def rmsnorm_fm(goff, TT):
        pb, pc_ = bank()
        for c in range(NCH):
            i = nxt("sq", 4)
            E("act", lambda e, c=c, i=i: e.activation(out=sq[i][:, :TT], in_=xT[:, c, :TT], func=AF.Square), reads=[xT_c[c]], writes=[sq_c[i]])
            E("pe", lambda e, c=c, i=i: e.matmul(pb[:, :TT], ones_bf[:], sq[i][:, :TT], start=(c == 0), stop=(c == NCH - 1)),
              reads=[sq_c[i], c_ones], writes=[pc_])
        rsqrt_tile(pb, pc_, 1.0 / D, TT, rstd, c_rstd)
        for c in range(NCH):
            E("dve", lambda e, c=c: e.scalar_tensor_tensor(out=hT[:, c, :TT], in0=xT[:, c, :TT], scalar=gains[:, goff + c:goff + c + 1],
                                                           in1=rstd[:, :TT], op0=ALU.mult, op1=ALU.mult),
              reads=[xT_c[c], c_gains, c_rstd], writes=[hT_c[c]])

    def ffn(goff, kg, ku, kd, TT):
        rmsnorm_fm(goff, TT)
        for fg in range(11):
            wg, wgc = next_tile((kg, "A", fg, 0))
            wu, wuc = next_tile((ku, "A", fg, 0))
            for jj in range(4):
                j = fg * 4 + jj
                pg, pgc = bank()
                pu, puc = bank()
                for (w_, wc_, pb, pbc) in ((wg, wgc, pg, pgc), (wu, wuc, pu, puc)):
                    for c in range(NCH):
                        E("pe", lambda e, w_=w_, pb=pb, c=c, jj=jj: e.matmul(pb[:, :TT], w_[:, c, jj * 128:(jj + 1) * 128], hT[:, c, :TT],
                                                                            start=(c == 0), stop=(c == NCH - 1)),
                          reads=[wc_, hT_c[c]], writes=[pbc])
                si = nxt("silu", 2)
                E("act", lambda e, pg=pg, si=si: e.activation(out=silu_t[si][:, :TT], in_=pg[:, :TT], func=AF.Silu), reads=[pgc], writes=[silu_c[si]])
                hap, hc = hid(j)
                E("dve", lambda e, pu=pu, si=si, hap=hap: e.tensor_tensor(out=hap[:, :TT], in0=silu_t[si][:, :TT], in1=pu[:, :TT], op=ALU.mult),
                  reads=[silu_c[si], puc], writes=hc)
        for dg in range(4):
            banks = [bank() for _ in range(4)]
            for pc in range(4):
                wd, wdc = next_tile((kd, "D", dg, pc))
                for jl in range(11):
                    j = pc * 11 + jl
                    hap, hc = hid(j)
                    for i in range(4):
                        pb, pbc = banks[i]
                        E("pe", lambda e, wd=wd, pb=pb, jl=jl, i=i, hap=hap, j=j: e.matmul(pb[:, :TT], wd[:, jl, i * 128:(i + 1) * 128], hap[:, :TT],
                                                                                        start=(j == 0), stop=(j == NF - 1)),
                          reads=[wdc] + hc, writes=[pbc])
            for i in range(4):
                pb, pbc = banks[i]
                c = dg * 4 + i
                E("dve", lambda e, pb=pb, c=c: e.scalar_tensor_tensor(out=xT[:, c, :TT], in0=pb[:, :TT], scalar=0.5, in1=xT[:, c, :TT],
                                                                     op0=ALU.mult, op1=ALU.add),
                  reads=[pbc, xT_c[c]], writes=[xT_c[c]])

    MAGIC = 12582912.0
    C1 = 6.28125
    C2 = 2.0 * math.pi - 6.28125

    def rope_tables(row0, TT):
        E("sp", lambda e: e.dma_start(out=posi[:, :TT], in_=pos_d[:, row0:row0 + TT].partition_broadcast(128)), writes=[c_posi], dsem=s_pos)
        a, k_, r = rt
        E("dve", lambda e: e.tensor_copy(out=a[:, :TT], in_=posi[:, :TT]), reads=[c_posi], writes=[rt_c[0]])
        E("dve", lambda e: e.tensor_scalar(out=a[:, :TT], in0=a[:, :TT], scalar1=freq[:, 0:1], scalar2=None, op0=ALU.mult),
          reads=[rt_c[0], c_freq], writes=[rt_c[0]])
        E("dve", lambda e: e.tensor_scalar(out=k_[:, :TT], in0=a[:, :TT], scalar1=1.0 / (2 * math.pi), scalar2=MAGIC, op0=ALU.mult, op1=ALU.add),
          reads=[rt_c[0]], writes=[rt_c[1]])
        E("dve", lambda e: e.tensor_scalar(out=k_[:, :TT], in0=k_[:, :TT], scalar1=MAGIC, scalar2=None, op0=ALU.subtract),
          reads=[rt_c[1]], writes=[rt_c[1]])
        E("dve", lambda e: e.scalar_tensor_tensor(out=r[:, :TT], in0=k_[:, :TT], scalar=-C1, in1=a[:, :TT], op0=ALU.mult, op1=ALU.add),
          reads=[rt_c[1], rt_c[0]], writes=[rt_c[2]])
        E("dve", lambda e: e.scalar_tensor_tensor(out=r[:, :TT], in0=k_[:, :TT], scalar=-C2, in1=r[:, :TT], op0=ALU.mult, op1=ALU.add),
          reads=[rt_c[1], rt_c[2]], writes=[rt_c[2]])
        E("dve", lambda e: e.tensor_scalar(out=r[:, :TT], in0=r[:, :TT], scalar1=-3.14159, scalar2=3.14159, op0=ALU.max, op1=ALU.min),
          reads=[rt_c[2]], writes=[rt_c[2]])
        E("act", lambda e: e.activation(out=SIN[:, :TT], in_=r[:, :TT], func=AF.Sin, scale=freq[:, 1:2]), reads=[rt_c[2], c_freq], writes=[c_sin])
        E("dve", lambda e: e.scalar_tensor_tensor(out=a[:, :TT], in0=r[:, :TT], scalar=-1.0, in1=r[:, :TT], op0=ALU.mult, op1=ALU.max),
          reads=[rt_c[2]], writes=[rt_c[0]])
        E("act", lambda e: e.activation(out=COS[:, :TT], in_=a[:, :TT], func=AF.Sin, scale=-1.0, bias=cst[:, 1:2]), reads=[rt_c[0], c_cst], writes=[c_cos])

    def qk_stage_a(cx):
        src, src_c, TT = cx["src"], cx["src_c"], cx["TT"]
        i = nxt("sq", 4)
        E("act", lambda e: e.activation(out=sq[i][:, :TT], in_=src[:, :TT], func=AF.Square), reads=[src_c], writes=[sq_c[i]])
        p2, p2c = bank(hold=True)
        E("pe", lambda e: e.matmul(p2[:, :TT], bd_bf[:], sq[i][:, :TT], start=True, stop=True), reads=[sq_c[i], c_bd], writes=[p2c])
        cx["p2"], cx["p2c"] = p2, p2c

    def qk_stage_b(cx):
        src, src_c, TT, gcol = cx["src"], cx["src_c"], cx["TT"], cx["gcol"]
        p2, p2c = cx["p2"], cx["p2c"]
        ts = nxt("tset", 2)
        A_, Ac = tA[ts], tA_c[ts]
        B_, Bc = tB[ts], tB_c[ts]
        Q_, Qc = qn[ts], qn_c[ts]
        E("act", lambda e: e.activation(out=A_[:, :TT], in_=p2[:, :TT], func=AF.Ln, bias=cst[:, 0:1], scale=1.0 / 64), reads=[p2c, c_cst], writes=[Ac])
        E("act", lambda e: e.activation(out=B_[:, :TT], in_=A_[:, :TT], func=AF.Exp, scale=-0.5), reads=[Ac], writes=[Bc])
        E("dve", lambda e: e.scalar_tensor_tensor(out=Q_[:, :TT], in0=src[:, :TT], scalar=gqk[:, gcol:gcol + 1], in1=B_[:, :TT], op0=ALU.mult, op1=ALU.mult),
          reads=[src_c, c_gqk, Bc], writes=[Qc])
        release(p2c)
        p3, p3c = bank(hold=True)
        E("pe", lambda e: e.matmul(p3[:, :TT], perm_bf[:], Q_[:, :TT], start=True, stop=True), reads=[Qc, c_perm], writes=[p3c])
        cx.update(A_=A_, Ac=Ac, B_=B_, Bc=Bc, Q_=Q_, Qc=Qc, p3=p3, p3c=p3c)

    def qk_stage_c(cx):
        TT, dst, dst_cells = cx["TT"], cx["dst"], cx["dst_cells"]
        A_, Ac, B_, Bc, Q_, Qc, p3, p3c = (cx[k] for k in ("A_", "Ac", "B_", "Bc", "Q_", "Qc", "p3", "p3c"))
        E("dve", lambda e: e.tensor_tensor(out=A_[:, :TT], in0=Q_[:, :TT], in1=COS[:, :TT], op=ALU.mult), reads=[Qc, c_cos], writes=[Ac])
        E("dve", lambda e: e.tensor_tensor(out=B_[:, :TT], in0=p3[:, :TT], in1=SIN[:, :TT], op=ALU.mult), reads=[p3c, c_sin], writes=[Bc])
        E("dve", lambda e: e.tensor_tensor(out=dst, in0=A_[:, :TT], in1=B_[:, :TT], op=ALU.add), reads=[Ac, Bc], writes=dst_cells)
        release(p3c)
        release(cx["src_c"])

    def qk_norm_rope(src, src_c, gcol, TT, dst, dst_cells):
        cx = dict(src=src, src_c=src_c, gcol=gcol, TT=TT, dst=dst, dst_cells=dst_cells)
        qk_stage_a(cx)
        qk_stage_b(cx)
        qk_stage_c(cx)

    def proj_fm(w_, wc_, col0, TT, hold=False):
        pb, pbc = bank(hold=hold)
        for c in range(NCH):
            E("pe", lambda e, c=c: e.matmul(pb[:, :TT], w_[:, c, col0:col0 + 128], hT[:, c, :TT], start=(c == 0), stop=(c == NCH - 1)),
              reads=[wc_, hT_c[c]], writes=[pbc])
        return pb, pbc

    def kv_proj(TT, blk0):
        w_, wc_ = next_tile(("in", "A", 2, 0))
        nb = TT // 128
        for kc in range(2):
            pb, pbc = proj_fm(w_, wc_, kc * 128, TT)
            qk_norm_rope(pb, pbc, 1, TT, kT[:, kc, blk0 * 128: blk0 * 128 + TT], kT_c[blk0:blk0 + nb])
        for b in range(nb):
            pb, pbc = bank()
            for c in range(NCH):
                E("pe", lambda e, c=c, b=b, pb=pb: e.matmul(pb[:, 0:256], hT[:, c, b * 128:(b + 1) * 128], w_[:, c, 256:512], start=(c == 0), stop=(c == NCH - 1)),
                  reads=[wc_, hT_c[c]], writes=[pbc])
            E("act", lambda e, b=b, pb=pb: e.activation(out=Vaug[:, blk0 + b, :, 0:64], in_=pb[:, 0:256].rearrange("p (h d) -> p h d", h=4), func=AF.Copy),
              reads=[pbc], writes=[V_c[blk0 + b]])

    SQC = math.sqrt(0.044715)
    GC = 2.0 * math.sqrt(2.0 / math.pi)

    def gelu_from_psum(pb, pbc, n, dst, dst_cells):
        ts = nxt("tset", 2)
        A_, Ac = tA[ts], tA_c[ts]
        E("act", lambda e: e.activation(out=A_[:, :n], in_=pb[:, :n], func=AF.Square, scale=SQC), reads=[pbc], writes=[Ac])
        E("dve", lambda e: e.scalar_tensor_tensor(out=A_[:, :n], in0=A_[:, :n], scalar=1.0, in1=pb[:, :n], op0=ALU.add, op1=ALU.mult),
          reads=[Ac, pbc], writes=[Ac])
        E("act", lambda e: e.activation(out=A_[:, :n], in_=A_[:, :n], func=AF.Sigmoid, scale=GC), reads=[Ac], writes=[Ac])
        E("dve", lambda e: e.tensor_tensor(out=dst, in0=A_[:, :n], in1=pb[:, :n], op=ALU.mult), reads=[Ac, pbc], writes=dst_cells)

    def mixer2(first_block_of_core):
        TT = T
        rmsnorm_fm(16, TT)
        pend = []

        def push(cx):
            pend.append(cx)
            n = len(pend)
            if n >= 2:
                qk_stage_a(pend[n - 2])
            if n >= 3:
                qk_stage_b(pend[n - 3])
                qk_stage_c(pend[n - 3])

        wkv, wkvc = next_tile(("in", "A", 2, 0))
        for kc in range(2):
            pb, pbc = proj_fm(wkv, wkvc, kc * 128, TT, hold=True)
            push(dict(src=pb, src_c=pbc, gcol=1, TT=TT, dst=kT[:, kc, 128:128 + TT], dst_cells=kT_c[1:1 + NB]))
        for b in range(NB):
            pb, pbc = bank()
            for c in range(NCH):
                E("pe", lambda e, c=c, b=b, pb=pb: e.matmul(pb[:, 0:256], hT[:, c, b * 128:(b + 1) * 128], wkv[:, c, 256:512], start=(c == 0), stop=(c == NCH - 1)),
                  reads=[wkvc, hT_c[c]], writes=[pbc])
            E("act", lambda e, b=b, pb=pb: e.activation(out=Vaug[:, 1 + b, :, 0:64], in_=pb[:, 0:256].rearrange("p (h d) -> p h d", h=4), func=AF.Copy),
              reads=[pbc], writes=[V_c[1 + b]])
        for g in range(2):
            w_, wc_ = next_tile(("in", "A", g, 0))
            for m in range(4):
                qm = g * 4 + m
                pb, pbc = proj_fm(w_, wc_, m * 128, TT, hold=True)
                qa, qc = qT(qm)
                push(dict(src=pb, src_c=pbc, gcol=0, TT=TT, dst=qa, dst_cells=qc))
        n = len(pend)
        qk_stage_a(pend[n - 1])
        qk_stage_b(pend[n - 2]); qk_stage_c(pend[n - 2])
        qk_stage_b(pend[n - 1]); qk_stage_c(pend[n - 1])
        if first_block_of_core:
            dump("b", 0, qT(0)[0], qT(0)[1])
            dump("b", 512, kT[:, 0, :], kT_c)
            dump("b", 1152, Vaug[:].rearrange("p b h d -> p (b h d)"), V_c)
            dump("f", 1024, COS[:, :], [c_cos])
            dump("f", 1536, SIN[:, :], [c_sin])
        if mix_level < 2 and not DBG:
            return
        tiles_ = {}

        def get_tile(cg):
            if cg not in tiles_:
                tiles_[cg] = next_tile(("in", "A", cg, 0))
            return tiles_[cg]

        def gv_mm(gt, b):
            w_, wc_ = get_tile(5 + gt)
            pb, pbc = bank()
            for c in range(NCH):
                E("pe", lambda e, c=c, b=b, pb=pb, w_=w_: e.matmul(pb, hT[:, c, b * 128:(b + 1) * 128], w_[:, c, :], start=(c == 0), stop=(c == NCH - 1)),
                  reads=[wc_, hT_c[c]], writes=[pbc])
            return (pb, pbc)

        def gv_chain(gt, b, ctx):
            pb, pbc = ctx
            ts = nxt("tset", 2)
            B_, Bc = tB[ts], tB_c[ts]
            gelu_from_psum(pb, pbc, 512, B_[:, :512], [Bc])
            i2 = nxt("tset", 2)
            A2, A2c = tA[i2], tA_c[i2]
            E("act", lambda e, A2=A2, B_=B_: e.activation(out=A2[:, :512], in_=B_[:, :512], func=AF.Square), reads=[Bc], writes=[A2c])
            smi = nxt("small", 4)
            s4 = small[:, smi * 16: smi * 16 + 4]
            s4c = [small_c[smi * 2], small_c[smi * 2 + 1]]
            E("dve", lambda e, A2=A2, s4=s4: e.tensor_reduce(out=s4, in_=A2[:, :512].rearrange("p (g c) -> p g c", g=4), axis=AX.X, op=ALU.add),
              reads=[A2c], writes=s4c)
            E("act", lambda e, s4=s4: e.activation(out=s4, in_=s4, func=AF.Sqrt, bias=cst[:, 0:1], scale=1.0 / 128), reads=s4c + [c_cst], writes=s4c)
            E("dve", lambda e, s4=s4: e.reciprocal(out=s4, in_=s4), reads=s4c, writes=s4c)
            ga, gc_ = gvn(b)
            E("dve", lambda e, ga=ga, gt=gt, B_=B_, s4=s4: e.tensor_tensor(
                out=ga[:, gt * 512:(gt + 1) * 512].rearrange("p (g c) -> p g c", g=4), in0=B_[:, :512].rearrange("p (g c) -> p g c", g=4),
                in1=s4.unsqueeze(2).broadcast_to([128, 4, 128]), op=ALU.mult),
              reads=[Bc] + s4c, writes=gc_)

        def gu_mm(gt, m):
            w_, wc_ = get_tile(3 + gt)
            return proj_fm(w_, wc_, m * 128, TT)

        def gu_chain(gt, m, ctx):
            pb, pbc = ctx
            ua, uc = gu(gt * 4 + m)
            gelu_from_psum(pb, pbc, TT, ua, uc)

        groups = [(gv_mm, gv_chain, gt, b) for gt in range(2) for b in range(NB)] + [(gu_mm, gu_chain, gt, m) for gt in range(2) for m in range(4)]
        gi = [0]

        def group_mm():
            if gi[0] < len(groups):
                fm, fc, a0, a1 = groups[gi[0]]
                gi[0] += 1
                ctx = fm(a0, a1)
                return (fc, a0, a1, ctx)
            return None

        def group_chain(g):
            if g is not None:
                fc, a0, a1, ctx = g
                fc(a0, a1, ctx)

        def emit_scores(b, qp):
            qms = (2 * qp, 2 * qp + 1)
            kc = qp // 2
            sbanks = [bank(), bank()]
            for idx, qm in enumerate(qms):
                qa, qc = qT(qm)
                for hh in range(2):
                    off = hh * 64
                    sb_, sbc = sbanks[hh]
                    for kk in range(2):
                        kb = b + kk
                        E("pe", lambda e, off=off, kk=kk, kb=kb, idx=idx, qa=qa, sb_=sb_, kc=kc, b=b: e.matmul(
                            sb_[:, idx * 256 + kk * 128: idx * 256 + (kk + 1) * 128],
                            kT[off:off + 64, kc, kb * 128:(kb + 1) * 128], qa[off:off + 64, b * 128:(b + 1) * 128], start=True, stop=True),
                          reads=[kT_c[kb]] + qc, writes=[sbc])
            return sbanks

        def emit_exp_mask(b, sbanks):
            mk, mkc = (mask0_bf, c_mask0) if (first_block_of_core and b == 0) else (mask_bf, c_mask)
            pts = []
            for hh in range(2):
                sb_, sbc = sbanks[hh]
                pi = nxt("pt", 4)
                pt, ptc = ptb(pi)
                pts.append((pt, ptc))
                E("act", lambda e, sb_=sb_, pt=pt: e.activation(out=pt, in_=sb_, func=AF.Exp), reads=[sbc], writes=ptc)
                E("dve", lambda e, pt=pt, mk=mk: e.tensor_tensor(out=pt, in0=pt, in1=mk[:], op=ALU.mult), reads=ptc + [mkc], writes=ptc)
            return pts

        def emit_pv(b, qp, pts, abanks):
            for idx, qm in enumerate((2 * qp, 2 * qp + 1)):
                for hh in range(2):
                    pt, ptc = pts[hh]
                    h = 8 * (qm // 4) + (qm % 4) + 4 * hh
                    kvh = 2 * (qm // 4) + hh
                    ab, abc = abanks[h // 6]
                    col = (h % 6) * 65
                    for kk in range(2):
                        kb = b + kk
                        E("pe", lambda e, ab=ab, col=col, pt=pt, idx=idx, kk=kk, kb=kb, kvh=kvh: e.matmul(
                            ab[:, col:col + 65], pt[:, idx * 256 + kk * 128: idx * 256 + (kk + 1) * 128], Vaug[:, kb, kvh, :],
                            start=(kk == 0), stop=(kk == 1)),
                          reads=ptc + [V_c[kb]], writes=[abc])

        deferred_tr = []
        pts_carry = None
        for b in range(NB):
            ao, aoc = aout(b % 2)
            abanks = [bank(hold=True) for _ in range(3)]
            pts_cur = pts_carry if b > 0 else emit_exp_mask(b, emit_scores(b, 0))
            for qp in range(4):
                if qp < 3:
                    pts_next = emit_exp_mask(b, emit_scores(b, qp + 1))
                elif b + 1 < NB:
                    pts_next = emit_exp_mask(b + 1, emit_scores(b + 1, 0))
                    pts_carry = pts_next
                else:
                    pts_next = None
                g_ = group_mm()
                emit_pv(b, qp, pts_cur, abanks)
                group_chain(g_)
                pts_cur = pts_next
                if qp == 0 and deferred_tr:
                    deferred_tr.pop(0)()
            smi = nxt("small", 4)
            den = small[:, smi * 16: smi * 16 + 16]
            denc = [small_c[smi * 2], small_c[smi * 2 + 1]]
            for bi in range(3):
                ab, abc = abanks[bi]
                n = 6 if bi < 2 else 4
                E("dve", lambda e, ab=ab, n=n, bi=bi, den=den: e.tensor_tensor(
                    out=den[:, bi * 6: bi * 6 + n].unsqueeze(2), in0=ab[:, 0:n * 65].rearrange("p (h d) -> p h d", d=65)[:, :, 64:65],
                    in1=esink[:, bi * 6: bi * 6 + n].unsqueeze(2), op=ALU.add),
                  reads=[abc, c_esink], writes=denc)
            E("dve", lambda e, den=den: e.reciprocal(out=den, in_=den), reads=denc, writes=denc)
            for bi in range(3):
                ab, abc = abanks[bi]
                n = 6 if bi < 2 else 4
                E("dve", lambda e, ab=ab, n=n, bi=bi, den=den, ao=ao: e.tensor_tensor(
                    out=ao[:, bi * 384: bi * 384 + n * 64].rearrange("p (h d) -> p h d", d=64),
                    in0=ab[:, 0:n * 65].rearrange("p (h d) -> p h d", d=65)[:, :, 0:64],
                    in1=den[:, bi * 6: bi * 6 + n].unsqueeze(2).broadcast_to([128, n, 64]), op=ALU.mult),
                  reads=[abc] + denc, writes=aoc)
            for (_, abc_) in abanks:
                release(abc_)
            smj = nxt("small", 4)
            ss = small[:, smj * 16: smj * 16 + 1]
            ssc = [small_c[smj * 2], small_c[smj * 2 + 1]]
            E("dve", lambda e, ao=ao: e.tensor_tensor(out=junk, in0=ao, in1=ao, op=ALU.mult), reads=aoc, writes=[c_junk])
            E("dve", lambda e, ss=ss: e.tensor_reduce(out=ss, in_=junk.unsqueeze(1), axis=AX.X, op=ALU.add), reads=[c_junk], writes=ssc)
            E("act", lambda e, ss=ss: e.activation(out=ss, in_=ss, func=AF.Sqrt, bias=cst[:, 0:1], scale=1.0 / 1024), reads=ssc + [c_cst], writes=ssc)
            E("dve", lambda e, ss=ss: e.reciprocal(out=ss, in_=ss), reads=ssc, writes=ssc)
            E("dve", lambda e, ao=ao, ss=ss: e.tensor_scalar(out=ao, in0=ao, scalar1=ss, scalar2=None, op0=ALU.mult), reads=aoc + ssc, writes=aoc)
            def do_tr(b=b, ao=ao, aoc=aoc):
                for half in range(2):
                    tb, tbc = bank()
                    for k in range(4):
                        m = half * 4 + k
                        E("pe", lambda e, tb=tb, k=k, m=m, ao=ao: e.transpose(tb[:, k * 128:(k + 1) * 128], ao[:, m * 128:(m + 1) * 128], ident[:]),
                          reads=aoc + [c_ident], writes=[tbc])
                    E("dve", lambda e, tb=tb, half=half, b=b: e.tensor_tensor(
                        out=mixA[:, half * 4: half * 4 + 4, b * 128:(b + 1) * 128], in0=tb.rearrange("p (k t) -> p k t", k=4),
                        in1=g8[:, 8 + half * 4: 8 + half * 4 + 4].unsqueeze(2).broadcast_to([128, 4, 128]), op=ALU.mult),
                      reads=[tbc, c_g8], writes=mixA_c[half * 4: half * 4 + 4])
            deferred_tr.append(do_tr)
        while deferred_tr:
            deferred_tr.pop(0)()
        while gi[0] < len(groups):
            group_chain(group_mm())
        if mix_level < 4 and not DBG:
            return
        for g in range(8):
            pb, pbc = bank()
            for b in range(NB):
                ga, gc_ = gvn(b)
                E("pe", lambda e, pb=pb, b=b, g=g, ga=ga: e.matmul(pb[:, b * 128:(b + 1) * 128], ga[:, g * 128:(g + 1) * 128], wsT[:, g, :], start=True, stop=True),
                  reads=gc_ + [c_wsT], writes=[pbc])
            ts = nxt("tset", 2)
            A_, Ac = tA[ts], tA_c[ts]
            E("dve", lambda e, pb=pb, g=g, A_=A_: e.scalar_tensor_tensor(
                out=A_[:, :TT].rearrange("p (b t) -> p b t", b=NB), in0=pb.rearrange("p (b t) -> p b t", b=NB), scalar=g8[:, g:g + 1],
                in1=bias_bc[:, g, :].unsqueeze(1).broadcast_to([128, NB, 128]), op0=ALU.mult, op1=ALU.add),
              reads=[pbc, c_g8, c_bias], writes=[Ac])
            ua, uc = gu(g)
            E("dve", lambda e, ua=ua, A_=A_: e.tensor_tensor(out=ua, in0=ua, in1=A_[:, :TT], op=ALU.mult), reads=uc + [Ac], writes=uc)
        pb, pbc = bank()
        for g in range(8):
            ua, uc = gu(g)
            i = nxt("sq", 4)
            E("act", lambda e, ua=ua, i=i: e.activation(out=sq[i][:, :TT], in_=ua, func=AF.Square), reads=uc, writes=[sq_c[i]])
            E("pe", lambda e, i=i, g=g, pb=pb: e.matmul(pb[:, :TT], ones_bf[:], sq[i][:, :TT], start=(g == 0), stop=(g == 7)), reads=[sq_c[i], c_ones], writes=[pbc])
        rsqrt_tile(pb, pbc, 1.0 / 1024, TT, rstd, c_rstd)
        for g in range(8):
            ua, uc = gu(g)
            E("dve", lambda e, ua=ua, g=g: e.scalar_tensor_tensor(out=hT[:, 8 + g, :TT], in0=ua, scalar=g8[:, 16 + g:17 + g], in1=rstd[:, :TT], op0=ALU.mult, op1=ALU.mult),
              reads=uc + [c_g8, c_rstd], writes=[hT_c[8 + g]])
        for dg in range(4):
            w_, wc_ = next_tile(("out", "A", dg, 0))
            banks = [bank() for _ in range(4)]
            for c in range(NCH):
                for i in range(4):
                    pb, pbc = banks[i]
                    src_ = mixA[:, c, :TT] if c < 8 else hT[:, c, :TT]
                    srcc_ = mixA_c[c] if c < 8 else hT_c[c]
                    E("pe", lambda e, pb=pb, c=c, i=i, w_=w_, src_=src_: e.matmul(pb[:, :TT], w_[:, c, i * 128:(i + 1) * 128], src_, start=(c == 0), stop=(c == NCH - 1)),
                      reads=[wc_, srcc_], writes=[pbc])
            for i in range(4):
                pb, pbc = banks[i]
                c = dg * 4 + i
                E("dve", lambda e, pb=pb, c=c: e.tensor_tensor(out=xT[:, c, :TT], in0=pb[:, :TT], in1=xT[:, c, :TT], op=ALU.add),
                  reads=[pbc, xT_c[c]], writes=[xT_c[c]])
        E("dve", lambda e: e.tensor_copy(out=kT[:, :, 0:128], in_=kT[:, :, NB * 128:(NB + 1) * 128]), reads=[kT_c[NB]], writes=[kT_c[0]])
        E("dve", lambda e: e.tensor_copy(out=Vaug[:, 0, :, :], in_=Vaug[:, NB, :, :]), reads=[V_c[NB]], writes=[V_c[0]])

    def program():
        if do_mixer:
            load_x(0, HALO)
            rope_tables(0, HALO)
            ffn(0, "g1", "u1", "d1", HALO)
            rmsnorm_fm(16, HALO)
            kv_proj(HALO, 0)
        for p in range(npass):
            row0 = HALO + p * T
            load_x(row0, T)
            if do_mixer:
                rope_tables(row0, T)
            if do_ffn1:
                ffn(0, "g1", "u1", "d1", T)
            if do_mixer:
                mixer2(p == 0)
            if do_ffn2:
                ffn(32, "g2", "u2", "d2", T)
            store_x(p * T, T)

    dry[0] = True
    program()
    dry[0] = False
    bank_ctr[0] = 0
    held.clear()
    for k_ in rr:
        rr[k_] = 0
    del out_ops[:]
    for (key, kind, cg, pc) in stream:
        conv(key, cg, kind, pc)
    program()
    assert consumed[0] == len(stream)
    P.final_ops = out_ops + dbg_ops
    run_prog(nc, P, st)
    st.close()
    return nc


def host_constants():
    ident = np.eye(128, dtype=np.float32)
    bd = np.zeros((128, 128), np.float32)
    bd[:64, :64] = 1.0
    bd[64:, 64:] = 1.0
    perm = np.zeros((128, 128), np.float32)
    freq = np.zeros((128, 2), np.float32)
    inv_freq = (np.float32(500000.0) ** (-np.arange(0, 16, 2, dtype=np.float32) / np.float32(16))).astype(np.float32)
    for p in range(128):
        d = p % 64
        if d < 8:
            perm[p + 8, p] = 1.0
            freq[p, 0] = inv_freq[d]
            freq[p, 1] = -1.0
        elif d < 16:
            perm[p - 8, p] = 1.0
            freq[p, 0] = inv_freq[d - 8]
            freq[p, 1] = 1.0
        else:
            freq[p, 1] = 1.0
    j = np.arange(128)[:, None]
    i = np.arange(128)[None, :]
    mprev = (j > i).astype(np.float32)
    mcur = (j <= i).astype(np.float32)
    mask = np.concatenate([mprev, mcur, mprev, mcur], axis=1)
    mask0 = np.concatenate([np.zeros_like(mprev), mcur, np.zeros_like(mprev), mcur], axis=1)
    return dict(ident=ident, bd=bd, perm=perm, freq=freq, mask=mask, mask0=mask0)


def q_col_perm():
    cols = []
    for qm in range(8):
        ha = 8 * (qm // 4) + (qm % 4)
        for h in (ha, ha + 4):
            cols.extend(range(h * 64, (h + 1) * 64))
    return np.array(cols + list(range(1024, IN_COLS)), dtype=np.int64)


_NC_CACHE = {}


def run(inputs, npass=8, do_mixer=True, do_ffn2=True, do_ffn1=True, trace=False, mix_level=4):
    key = (npass, do_mixer, do_ffn2, do_ffn1, mix_level)
    if key not in _NC_CACHE:
        _NC_CACHE[key] = build_program(npass, do_mixer, do_ffn2, do_ffn1, mix_level)
    nc = _NC_CACHE[key]
    f = lambda a: np.ascontiguousarray(np.asarray(a, dtype=np.float32))
    x = np.asarray(inputs["x"], dtype=np.float32)
    pos = np.asarray(inputs["positions"], dtype=np.int32)
    cst = host_constants()
    w_in = f(np.asarray(inputs["w_in"])[0][:, q_col_perm()])
    shared = {
        "w_g1": f(inputs["ffn1_w_gate"][0]), "w_u1": f(inputs["ffn1_w_up"][0]), "w_d1": f(inputs["ffn1_w_down"][0]),
        "w_in": w_in, "w_out": f(inputs["w_out"][0]),
        "w_g2": f(inputs["ffn2_w_gate"][0]), "w_u2": f(inputs["ffn2_w_up"][0]), "w_d2": f(inputs["ffn2_w_down"][0]),
        "gains": f(np.concatenate([np.asarray(inputs[k])[0].reshape(16, 128).T for k in ("ffn1_norm", "mix_norm", "ffn2_norm")], axis=1)),
        "gqk": f(np.stack([np.tile(np.asarray(inputs["q_norm"])[0], 2), np.tile(np.asarray(inputs["k_norm"])[0], 2)], axis=1)),
        "g8": f(np.concatenate([np.asarray(inputs[k])[0].reshape(8, 128).T for k in ("gmlp_v_norm", "attn_out_norm", "gmlp_out_norm")], axis=1)),
        "sinks": f(np.asarray(inputs["attn_sinks"])[0].reshape(1, 16)),
        "wsT": f(np.asarray(inputs["gmlp_w_s"])[0].transpose(2, 0, 1).reshape(128, 1024)),
        "bs": f(np.asarray(inputs["gmlp_b_s"])[0].reshape(1, 1024)),
        "ident": cst["ident"], "bd": cst["bd"], "perm": cst["perm"], "mask": cst["mask"], "freq": cst["freq"],
    }
    in_maps = []
    for c in range(NCORES):
        b = c // 4
        s0 = (c % 4) * TOK_PER_CORE
        if s0 == 0:
            xc = np.concatenate([np.zeros((HALO, D), np.float32), x[b, 0:TOK_PER_CORE]], axis=0)
            pc = np.concatenate([np.zeros((HALO,), np.int32), pos[b, 0:TOK_PER_CORE]])
            m0 = cst["mask0"]
        else:
            xc = x[b, s0 - HALO: s0 + TOK_PER_CORE]
            pc = pos[b, s0 - HALO: s0 + TOK_PER_CORE]
            m0 = cst["mask"]
        d = dict(shared)
        d["x"] = np.ascontiguousarray(xc)
        d["pos"] = np.ascontiguousarray(pc.reshape(1, -1))
        d["mask0"] = m0
        in_maps.append(d)
    res = run_bass_kernel_spmd(nc, in_maps, core_ids=list(range(NCORES)), **({"trace": True} if trace else {}))
    out = np.empty((2, SEQ, D), np.float32)
    for c in range(NCORES):
        b = c // 4
        s0 = (c % 4) * TOK_PER_CORE
        out[b, s0:s0 + TOK_PER_CORE] = res.results[c]["y"]
    if mix_level >= 100:
        return out, res
    if trace:
        return out, res
    return out


def kernel(**inputs):
    return run(inputs)
```

```python
import math
from contextlib import ExitStack

import numpy as np
import concourse.bass as bass
import concourse.mybir as mybir
from concourse.bass_utils import run_bass_kernel_spmd

F32 = mybir.dt.float32
BF16 = mybir.dt.bfloat16
I32 = mybir.dt.int32
AF = mybir.ActivationFunctionType
ALU = mybir.AluOpType
AX = mybir.AxisListType

D = 2048
DFF = 5632
NCH = 16
NF = 44
T = 512
NB = T // 128
NCORES = 8
SEQ = 16384
TOK_PER_CORE = 4096
HALO = 128
IN_COLS = 3584
EPS = 1e-6
NSLOT = 4
COMPUTE = ("pe", "act", "dve", "pool")


class Cell:
    __slots__ = ("w", "r", "rd", "excl")

    def __init__(self, excl=False):
        self.w = None
        self.r = {}
        self.rd = []
        self.excl = excl


class Op:
    __slots__ = ("eng", "fn", "deps", "idx", "signal", "is_dma", "sem", "semval")

    def __init__(self, eng, fn, is_dma):
        self.eng = eng
        self.fn = fn
        self.deps = []
        self.signal = False
        self.is_dma = is_dma
        self.sem = None
        self.semval = 0
        self.idx = 0


class DSem:
    def __init__(self, name):
        self.name = name
        self.handle = None
        self.count = 0


class Prog:
    def __init__(self):
        self.ops = {e: [] for e in ("pe", "act", "dve", "pool", "sp")}
        self.dsems = []
        self.final_ops = []

    def dsem(self, name):
        s = DSem(name)
        self.dsems.append(s)
        return s

    def emit(self, eng, fn, reads=(), writes=(), dsem=None):
        is_dma = dsem is not None
        op = Op(eng, fn, is_dma)
        lst = self.ops[eng]
        op.idx = len(lst)
        deps = {}
        if any(c.excl for c in reads):
            reads = list(reads)
            writes = list(writes) + [c for c in reads if c.excl and c not in writes]
            reads = [c for c in reads if not c.excl]
        for c in reads:
            if c.w is not None:
                deps[id(c.w)] = c.w
        for c in writes:
            if c.w is not None:
                deps[id(c.w)] = c.w
            for o in c.r.values():
                deps[id(o)] = o
            for o in c.rd:
                deps[id(o)] = o
        for c in reads:
            if is_dma:
                c.rd.append(op)
            else:
                c.r[eng] = op
        for c in writes:
            c.w = op
            c.r = {}
            c.rd = []
        op.deps = list(deps.values())
        if is_dma:
            op.sem = dsem
            dsem.count += 16
            op.semval = dsem.count
        lst.append(op)
        return op

    def finalize(self):
        for e, lst in self.ops.items():
            for op in lst:
                nd = []
                for d in op.deps:
                    if (not d.is_dma) and (not op.is_dma) and d.eng == "pe" and e == "pe":
                        continue
                    nd.append(d)
                    d.signal = True
                op.deps = nd
        for op in self.final_ops:
            op.signal = True
        for e in COMPUTE:
            n = 0
            for op in self.ops[e]:
                if op.is_dma:
                    continue
                if op.signal:
                    n += 1
                    op.semval = n


def run_prog(nc, prog, st):
    prog.finalize()
    csem = {e: st.enter_context(nc.semaphore("c_" + e)) for e in COMPUTE}
    for s in prog.dsems:
        s.handle = st.enter_context(nc.semaphore("d_" + s.name))
    block = st.enter_context(nc.Block())

    def keyof(d):
        if d.is_dma:
            return ("d", id(d.sem)), d.sem.handle
        return ("c", d.eng), csem[d.eng]

    def replay(ename, eng):
        seen = {}
        for op in prog.ops[ename]:
            for d in op.deps:
                key, h = keyof(d)
                if seen.get(key, 0) >= d.semval:
                    continue
                seen[key] = d.semval
                eng.wait_ge(h, d.semval)
            ins = op.fn(eng)
            if op.is_dma:
                ins.then_inc(op.sem.handle, 16)
            elif op.signal:
                ins.then_inc(csem[op.eng], 1)
        if ename == "sp":
            for d in prog.final_ops:
                key, h = keyof(d)
                if seen.get(key, 0) >= d.semval:
                    continue
                seen[key] = d.semval
                eng.wait_ge(h, d.semval)

    @block.tensor
    def _(eng):
        replay("pe", eng)

    @block.scalar
    def _(eng):
        replay("act", eng)

    @block.vector
    def _(eng):
        replay("dve", eng)

    @block.gpsimd
    def _(eng):
        replay("pool", eng)

    @block.sync
    def _(eng):
        replay("sp", eng)


def build_program(npass=8, do_mixer=True, do_ffn2=True, do_ffn1=True, mix_level=4):
    nc = bass.Bass("TRN2", target_bir_lowering=False)
    P = Prog()
    st = ExitStack()

    def din(name, shape, dt=F32):
        return nc.dram_tensor(name, list(shape), dt, kind="ExternalInput").ap()

    ntok_in = HALO + TOK_PER_CORE
    x_d = din("x", [ntok_in, D])
    pos_d = din("pos", [1, ntok_in], I32)
    y_d = nc.dram_tensor("y", [TOK_PER_CORE, D], F32, kind="ExternalOutput").ap()
    wsrc = {
        "g1": din("w_g1", [D, DFF]), "u1": din("w_u1", [D, DFF]), "d1": din("w_d1", [DFF, D]),
        "in": din("w_in", [D, IN_COLS]), "out": din("w_out", [D, D]),
        "g2": din("w_g2", [D, DFF]), "u2": din("w_u2", [D, DFF]), "d2": din("w_d2", [DFF, D]),
    }
    wbf = {k: nc.dram_tensor("wb_" + k, list(v.shape), BF16, kind="Internal").ap() for k, v in wsrc.items()}
    gains_d = din("gains", [128, 48])
    gqk_d = din("gqk", [128, 2])
    g8_d = din("g8", [128, 24])
    sinks_d = din("sinks", [1, 16])
    wsT_d = din("wsT", [128, 8 * 128])
    bs_d = din("bs", [1, 8 * 128])
    ident_d = din("ident", [128, 128])
    bd_d = din("bd", [128, 128])
    perm_d = din("perm", [128, 128])
    mask_d = din("mask", [128, 512])
    mask0_d = din("mask0", [128, 512])
    freq_d = din("freq", [128, 2])

    DBG = mix_level >= 100
    if DBG:
        dbg_f = nc.dram_tensor("dbg_f", [128, 4096], F32, kind="ExternalOutput").ap()
        dbg_b = nc.dram_tensor("dbg_b", [128, 4096], BF16, kind="ExternalOutput").ap()
    dbg_ops = []

    def dump(kind, col0, src, cells_):
        if not DBG or dry[0]:
            return
        n = src.shape[-1]
        dst = (dbg_f if kind == "f" else dbg_b)[:, col0:col0 + n]
        dbg_ops.append(P.emit("sp", lambda e: e.dma_start(out=dst, in_=src), reads=cells_, dsem=s_dbg))

    def sb(name, shape, dt=F32):
        return st.enter_context(nc.sbuf_tensor("s_" + name, list(shape), dt))

    def cells(n):
        return [Cell() for _ in range(n)]

    xT = sb("xT", [128, NCH, T]); xT_c = cells(NCH)
    hT = sb("hT", [128, NCH, T], BF16); hT_c = cells(NCH)
    mixA = sb("mixA", [128, 8, T], BF16); mixA_c = cells(8)
    big = sb("big", [128, NF * T // 2])
    big_c = cells(NF)
    big_bf = big[:].bitcast(BF16)
    wsl = sb("wsl", [128, NSLOT, 8192], BF16); wsl_c = cells(NSLOT)
    kT = sb("kT", [128, 2, (NB + 1) * 128], BF16); kT_c = cells(NB + 1)
    Vaug = sb("Vaug", [128, NB + 1, 4, 65], BF16); V_c = cells(NB + 1)
    ident = sb("ident", [128, 128]); c_ident = Cell()
    ones_bf = sb("ones_bf", [128, 128], BF16); c_ones = Cell()
    bd_bf = sb("bd_bf", [128, 128], BF16); c_bd = Cell()
    perm_bf = sb("perm_bf", [128, 128], BF16); c_perm = Cell()
    mask_bf = sb("mask_bf", [128, 512], BF16); c_mask = Cell()
    mask0_bf = sb("mask0_bf", [128, 512], BF16); c_mask0 = Cell()
    wsT = sb("wsT", [128, 8, 128], BF16); c_wsT = Cell()
    bias_bc = sb("bias_bc", [128, 8, 128]); c_bias = Cell()
    gains = sb("gains", [128, 48]); c_gains = Cell()
    gqk = sb("gqk", [128, 2]); c_gqk = Cell()
    g8 = sb("g8", [128, 24]); c_g8 = Cell()
    esink = sb("esink", [128, 16]); c_esink = Cell()
    freq = sb("freq", [128, 2]); c_freq = Cell()
    cst = sb("cst", [128, 4]); c_cst = Cell()
    COS = sb("COS", [128, T]); c_cos = Cell()
    SIN = sb("SIN", [128, T]); c_sin = Cell()
    posi = sb("posi", [128, T], I32); c_posi = Cell()
    sq = [sb("sq%d" % i, [128, T], BF16) for i in range(4)]; sq_c = cells(4)
    rstd = sb("rstd", [128, T]); c_rstd = Cell()
    sdt = sb("sdt", [128, T]); c_sdt = Cell()
    silu_t = [sb("silu%d" % i, [128, T]) for i in range(2)]; silu_c = cells(2)
    tA = [sb("tA%d" % i, [128, T]) for i in range(2)]; tA_c = cells(2)
    tB = [sb("tB%d" % i, [128, T]) for i in range(2)]; tB_c = cells(2)
    qn = [sb("qn%d" % i, [128, T], BF16) for i in range(2)]; qn_c = cells(2)
    rt = [tA[0], tA[1], tB[0]]; rt_c = [tA_c[0], tA_c[1], tB_c[0]]
    ldtmp = tB[1]; c_ldtmp = tB_c[1]
    junk = sdt[:].bitcast(BF16); c_junk = c_sdt
    small = sb("small", [128, 64]); small_c = cells(8)

    ps = st.enter_context(nc.psum_tensor("ps", [128, 8 * 512], F32))
    ps_c = [Cell(excl=True) for _ in range(8)]
    bank_ctr = [0]

    held = set()

    def bank(hold=False):
        while True:
            b = bank_ctr[0] % 8
            bank_ctr[0] += 1
            if b not in held:
                break
        if hold:
            held.add(b)
        return ps[:, b * 512:(b + 1) * 512], ps_c[b]

    def release(pc_):
        held.discard(ps_c.index(pc_))

    rr = {"sq": 0, "silu": 0, "tset": 0, "pt": 0, "stg": 0, "small": 0}

    def nxt(k, n):
        v = rr[k] % n
        rr[k] += 1
        return v

    def hid(j):
        return big_bf[:, j * T:(j + 1) * T], [big_c[j]]

    def qT(m):
        return big_bf[:, m * T:(m + 1) * T], [big_c[m]]

    def gu(g):
        return big[:, 4 * T + g * T: 4 * T + (g + 1) * T], [big_c[8 + 2 * g], big_c[9 + 2 * g]]

    def gvn(b):
        return big_bf[:, 24 * T + b * 1024: 24 * T + (b + 1) * 1024], [big_c[24 + 2 * b], big_c[25 + 2 * b]]

    def aout(i):
        o = 16 * T + i * 1024
        return big[:, o:o + 1024], big_c[32 + 4 * i: 36 + 4 * i]

    def ptb(i):
        return big_bf[:, (40 + i) * T:(41 + i) * T], [big_c[40 + i]]

    def stg(i):
        o = (28 + 8 * i) * (T // 2)
        return big[:, o:o + 2048], big_c[28 + 8 * i: 36 + 8 * i]

    s_const = P.dsem("const")
    s_slot = [P.dsem("slot%d" % i) for i in range(NSLOT)]
    s_xin = [P.dsem("xin%d" % i) for i in range(2)]
    s_xout = [P.dsem("xout%d" % i) for i in range(2)]
    s_pos = P.dsem("pos")
    s_dbg = P.dsem("dbg")

    dry = [False]

    def E(eng, fn, reads=(), writes=(), dsem=None):
        if dry[0]:
            return None
        return P.emit(eng, fn, reads, writes, dsem)

    nconst = [0]

    def load_const(dst_ap, src_ap, cell):
        nconst[0] += 1
        E("sp", lambda e: e.dma_start(out=dst_ap, in_=src_ap), writes=[cell], dsem=P.dsem("const%d" % nconst[0]))

    load_const(ident[:], ident_d, c_ident)
    load_const(gains[:], gains_d, c_gains)
    load_const(gqk[:], gqk_d, c_gqk)
    load_const(g8[:], g8_d, c_g8)
    load_const(freq[:], freq_d, c_freq)
    load_const(esink[:], sinks_d.partition_broadcast(128), c_esink)
    load_const(bias_bc[:].rearrange("p g t -> p (g t)"), bs_d.partition_broadcast(128), c_bias)
    E("dve", lambda e: e.memset(ones_bf[:], 1.0), writes=[c_ones])
    E("dve", lambda e: e.memset(cst[:, 0:1], EPS), writes=[c_cst])
    E("dve", lambda e: e.memset(cst[:, 1:2], math.pi / 2), writes=[c_cst])
    E("dve", lambda e: e.memset(Vaug[:].rearrange("p b h d -> p (b h d)"), 1.0), writes=V_c)
    E("dve", lambda e: e.memset(kT[:].rearrange("p c t -> p (c t)"), 0.0), writes=kT_c)
    E("act", lambda e: e.activation(out=esink[:], in_=esink[:], func=AF.Exp), reads=[c_esink], writes=[c_esink])
    E("dve", lambda e: e.tensor_scalar(out=gqk[:, 0:1], in0=gqk[:, 0:1], scalar1=0.125, scalar2=None, op0=ALU.mult),
      reads=[c_gqk], writes=[c_gqk])
    for (src, dst, cdst, n) in ((bd_d, bd_bf[:], c_bd, 128), (perm_d, perm_bf[:], c_perm, 128),
                                (mask_d, mask_bf[:], c_mask, 512), (mask0_d, mask0_bf[:], c_mask0, 512)):
        load_const(ldtmp[:, 0:n], src, c_ldtmp)
        E("dve", lambda e, dst=dst, n=n: e.tensor_copy(out=dst, in_=ldtmp[:, 0:n]), reads=[c_ldtmp], writes=[cdst])
    for hf in range(2):
        load_const(ldtmp[:, 0:512], wsT_d[:, hf * 512:(hf + 1) * 512], c_ldtmp)
        E("dve", lambda e, hf=hf: e.tensor_tensor(out=wsT[:, hf * 4:(hf + 1) * 4, :], in0=ldtmp[:, 0:512].rearrange("p (g t) -> p g t", g=4),
                                                  in1=mask_bf[:, 128:256].unsqueeze(1).broadcast_to([128, 4, 128]), op=ALU.mult),
          reads=[c_ldtmp, c_mask], writes=[c_wsT])

    conv_cells = {}
    conv_order = []

    def conv(key, cg, kind, pc):
        half = (pc // 2) if kind == "D" else 0
        ck = (key, cg, half)
        if ck in conv_cells:
            return
        c = Cell()
        conv_cells[ck] = c
        s = P.dsem("cv_%s_%d_%d" % (key, cg, half))
        if kind == "D":
            r0, r1 = half * (DFF // 2), (half + 1) * (DFF // 2)
        else:
            r0, r1 = 0, D
        src = wsrc[key][r0:r1, cg * 512:(cg + 1) * 512]
        dst = wbf[key][r0:r1, cg * 512:(cg + 1) * 512]
        conv_order.append(c)
        thr = [conv_order[-5]] if len(conv_order) >= 5 else []
        E("pool", lambda e: e.dma_start(out=dst, in_=src), reads=thr, writes=[c], dsem=s)

    def ffn_tiles(g, u, d):
        tl = []
        for fg in range(11):
            tl.append((g, "A", fg, 0))
            tl.append((u, "A", fg, 0))
        for dg in range(4):
            for pc in range(4):
                tl.append((d, "D", dg, pc))
        return tl

    stream = []
    loaded = [0]
    consumed = [0]

    def load_tile(i):
        key, kind, cg, pc = stream[i]
        s = i % NSLOT
        if kind == "A":
            src = wbf[key][:, cg * 512:(cg + 1) * 512].rearrange("(c p) n -> p c n", p=128)
            dst = wsl[:, s, :].rearrange("p (c n) -> p c n", n=512)
        else:
            src = wbf[key][pc * 11 * 128:(pc + 1) * 11 * 128, cg * 512:(cg + 1) * 512].rearrange("(c p) n -> p c n", p=128)
            dst = wsl[:, s, 0:11 * 512].rearrange("p (c n) -> p c n", n=512)
        E("sp", lambda e: e.dma_start(out=dst, in_=src), reads=[conv_cells[(key, cg, (pc // 2) if kind == "D" else 0)]], writes=[wsl_c[s]], dsem=s_slot[s])

    def next_tile(expect):
        if dry[0]:
            stream.append(expect)
            return wsl[:, 0, :].rearrange("p (c n) -> p c n", n=512), wsl_c[0]
        i = consumed[0]
        assert stream[i] == expect, (stream[i], expect)
        while loaded[0] < min(len(stream), i + NSLOT - 1):
            load_tile(loaded[0])
            loaded[0] += 1
        consumed[0] += 1
        s = i % NSLOT
        return wsl[:, s, :].rearrange("p (c n) -> p c n", n=512), wsl_c[s]

    def load_x(row0, TT):
        for b in range(TT // 128):
            si = nxt("stg", 2)
            sap, sc = stg(si)
            r0 = row0 + b * 128
            E("sp", lambda e, sap=sap, r0=r0: e.dma_start(out=sap, in_=x_d[r0:r0 + 128, :]), writes=sc, dsem=s_xin[si])
            for cgp in range(4):
                pb, pc_ = bank()
                for k in range(4):
                    c = cgp * 4 + k
                    E("pe", lambda e, pb=pb, sap=sap, c=c, k=k: e.transpose(pb[:, k * 128:(k + 1) * 128], sap[:, c * 128:(c + 1) * 128], ident[:]),
                      reads=sc + [c_ident], writes=[pc_])
                E("dve" if cgp % 2 else "act",
                  (lambda e, pb=pb, cgp=cgp, b=b: e.tensor_copy(out=xT[:, cgp * 4:cgp * 4 + 4, b * 128:(b + 1) * 128], in_=pb.rearrange("p (k t) -> p k t", k=4)))
                  if cgp % 2 else
                  (lambda e, pb=pb, cgp=cgp, b=b: e.activation(out=xT[:, cgp * 4:cgp * 4 + 4, b * 128:(b + 1) * 128], in_=pb.rearrange("p (k t) -> p k t", k=4), func=AF.Copy)),
                  reads=[pc_], writes=xT_c[cgp * 4:cgp * 4 + 4])

    out_ops = []

    def store_x(row0, TT):
        for b in range(TT // 128):
            si = nxt("stg", 2)
            sap, sc = stg(si)
            for cgp in range(4):
                pb, pc_ = bank()
                for k in range(4):
                    c = cgp * 4 + k
                    E("pe", lambda e, pb=pb, c=c, k=k, b=b: e.transpose(pb[:, k * 128:(k + 1) * 128], xT[:, c, b * 128:(b + 1) * 128], ident[:]),
                      reads=[xT_c[c], c_ident], writes=[pc_])
                if cgp % 2:
                    E("dve", lambda e, pb=pb, sap=sap, cgp=cgp: e.tensor_copy(out=sap[:, cgp * 512:(cgp + 1) * 512], in_=pb), reads=[pc_], writes=sc)
                else:
                    E("act", lambda e, pb=pb, sap=sap, cgp=cgp: e.activation(out=sap[:, cgp * 512:(cgp + 1) * 512], in_=pb, func=AF.Copy), reads=[pc_], writes=sc)
            r0 = row0 + b * 128
            op = E("sp", lambda e, sap=sap, r0=r0: e.dma_start(out=y_d[r0:r0 + 128, :], in_=sap), reads=sc, dsem=s_xout[si])
            out_ops.append(op)

    def rsqrt_tile(src_ps, src_cell, scale, TT, dst, dst_cell):
        E("act", lambda e: e.activation(out=sdt[:, :TT], in_=src_ps[:, :TT], func=AF.Ln, bias=cst[:, 0:1], scale=scale),
          reads=[src_cell, c_cst], writes=[c_sdt])
        E("act", lambda e: e.activation(out=dst[:, :TT], in_=sdt[:, :TT], func=AF.Exp, scale=-0.5), reads=[c_sdt], writes=[dst_cell])

    def rmsnorm_fm(goff, TT):
        pb, pc_ = bank()
        for c in range(NCH):
            i = nxt("sq", 4)
            E("act", lambda e, c=c, i=i: e.activation(out=sq[i][:, :TT], in_=xT[:, c, :TT], func=AF.Square), reads=[xT_c[c]], writes=[sq_c[i]])
            E("pe", lambda e, c=c, i=i: e.matmul(pb[:, :TT], ones_bf[:], sq[i][:, :TT], start=(c == 0), stop=(c == NCH - 1)),
              reads=[sq_c[i], c_ones], writes=[pc_])
        rsqrt_tile(pb, pc_, 1.0 / D, TT, rstd, c_rstd)
        for c in range(NCH):
            E("dve", lambda e, c=c: e.scalar_tensor_tensor(out=hT[:, c, :TT], in0=xT[:, c, :TT], scalar=gains[:, goff + c:goff + c + 1],
                                                           in1=rstd[:, :TT], op0=ALU.mult, op1=ALU.mult),
              reads=[xT_c[c], c_gains, c_rstd], writes=[hT_c[c]])

    def ffn(goff, kg, ku, kd, TT):
        rmsnorm_fm(goff, TT)
        for fg in range(11):
            wg, wgc = next_tile((kg, "A", fg, 0))
            wu, wuc = next_tile((ku, "A", fg, 0))
            for jj in range(4):
                j = fg * 4 + jj
                pg, pgc = bank()
                pu, puc = bank()
                for (w_, wc_, pb, pbc) in ((wg, wgc, pg, pgc), (wu, wuc, pu, puc)):
                    for c in range(NCH):
                        E("pe", lambda e, w_=w_, pb=pb, c=c, jj=jj: e.matmul(pb[:, :TT], w_[:, c, jj * 128:(jj + 1) * 128], hT[:, c, :TT],
                                                                            start=(c == 0), stop=(c == NCH - 1)),
                          reads=[wc_, hT_c[c]], writes=[pbc])
                si = nxt("silu", 2)
                E("act", lambda e, pg=pg, si=si: e.activation(out=silu_t[si][:, :TT], in_=pg[:, :TT], func=AF.Silu), reads=[pgc], writes=[silu_c[si]])
                hap, hc = hid(j)
                E("dve", lambda e, pu=pu, si=si, hap=hap: e.tensor_tensor(out=hap[:, :TT], in0=silu_t[si][:, :TT], in1=pu[:, :TT], op=ALU.mult),
                  reads=[silu_c[si], puc], writes=hc)
        for dg in range(4):
            banks = [bank() for _ in range(4)]
            for pc in range(4):
                wd, wdc = next_tile((kd, "D", dg, pc))
                for jl in range(11):
                    j = pc * 11 + jl
                    hap, hc = hid(j)
                    for i in range(4):
                        pb, pbc = banks[i]
                        E("pe", lambda e, wd=wd, pb=pb, jl=jl, i=i, hap=hap, j=j: e.matmul(pb[:, :TT], wd[:, jl, i * 128:(i + 1) * 128], hap[:, :TT],
                                                                                        start=(j == 0), stop=(j == NF - 1)),
                          reads=[wdc] + hc, writes=[pbc])
            for i in range(4):
                pb, pbc = banks[i]
                c = dg * 4 + i
                E("dve", lambda e, pb=pb, c=c: e.scalar_tensor_tensor(out=xT[:, c, :TT], in0=pb[:, :TT], scalar=0.5, in1=xT[:, c, :TT],
                                                                     op0=ALU.mult, op1=ALU.add),
                  reads=[pbc, xT_c[c]], writes=[xT_c[c]])

    MAGIC = 12582912.0
    C1 = 6.28125
    C2 = 2.0 * math.pi - 6.28125

    def rope_tables(row0, TT):
        E("sp", lambda e: e.dma_start(out=posi[:, :TT], in_=pos_d[:, row0:row0 + TT].partition_broadcast(128)), writes=[c_posi], dsem=s_pos)
        a, k_, r = rt
        E("dve", lambda e: e.tensor_copy(out=a[:, :TT], in_=posi[:, :TT]), reads=[c_posi], writes=[rt_c[0]])
        E("dve", lambda e: e.tensor_scalar(out=a[:, :TT], in0=a[:, :TT], scalar1=freq[:, 0:1], scalar2=None, op0=ALU.mult),
          reads=[rt_c[0], c_freq], writes=[rt_c[0]])
        E("dve", lambda e: e.tensor_scalar(out=k_[:, :TT], in0=a[:, :TT], scalar1=1.0 / (2 * math.pi), scalar2=MAGIC, op0=ALU.mult, op1=ALU.add),
          reads=[rt_c[0]], writes=[rt_c[1]])
        E("dve", lambda e: e.tensor_scalar(out=k_[:, :TT], in0=k_[:, :TT], scalar1=MAGIC, scalar2=None, op0=ALU.subtract),
          reads=[rt_c[1]], writes=[rt_c[1]])
        E("dve", lambda e: e.scalar_tensor_tensor(out=r[:, :TT], in0=k_[:, :TT], scalar=-C1, in1=a[:, :TT], op0=ALU.mult, op1=ALU.add),
          reads=[rt_c[1], rt_c[0]], writes=[rt_c[2]])
        E("dve", lambda e: e.scalar_tensor_tensor(out=r[:, :TT], in0=k_[:, :TT], scalar=-C2, in1=r[:, :TT], op0=ALU.mult, op1=ALU.add),
          reads=[rt_c[1], rt_c[2]], writes=[rt_c[2]])
        E("dve", lambda e: e.tensor_scalar(out=r[:, :TT], in0=r[:, :TT], scalar1=-3.14159, scalar2=3.14159, op0=ALU.max, op1=ALU.min),
          reads=[rt_c[2]], writes=[rt_c[2]])
        E("act", lambda e: e.activation(out=SIN[:, :TT], in_=r[:, :TT], func=AF.Sin, scale=freq[:, 1:2]), reads=[rt_c[2], c_freq], writes=[c_sin])
        E("dve", lambda e: e.scalar_tensor_tensor(out=a[:, :TT], in0=r[:, :TT], scalar=-1.0, in1=r[:, :TT], op0=ALU.mult, op1=ALU.max),
          reads=[rt_c[2]], writes=[rt_c[0]])
        E("act", lambda e: e.activation(out=COS[:, :TT], in_=a[:, :TT], func=AF.Sin, scale=-1.0, bias=cst[:, 1:2]), reads=[rt_c[0], c_cst], writes=[c_cos])

    def qk_stage_a(cx):
        src, src_c, TT = cx["src"], cx["src_c"], cx["TT"]
        i = nxt("sq", 4)
        E("act", lambda e: e.activation(out=sq[i][:, :TT], in_=src[:, :TT], func=AF.Square), reads=[src_c], writes=[sq_c[i]])
        p2, p2c = bank(hold=True)
        E("pe", lambda e: e.matmul(p2[:, :TT], bd_bf[:], sq[i][:, :TT], start=True, stop=True), reads=[sq_c[i], c_bd], writes=[p2c])
        cx["p2"], cx["p2c"] = p2, p2c

    def qk_stage_b(cx):
        src, src_c, TT, gcol = cx["src"], cx["src_c"], cx["TT"], cx["gcol"]
        p2, p2c = cx["p2"], cx["p2c"]
        ts = nxt("tset", 2)
        A_, Ac = tA[ts], tA_c[ts]
        B_, Bc = tB[ts], tB_c[ts]
        Q_, Qc = qn[ts], qn_c[ts]
        E("act", lambda e: e.activation(out=A_[:, :TT], in_=p2[:, :TT], func=AF.Ln, bias=cst[:, 0:1], scale=1.0 / 64), reads=[p2c, c_cst], writes=[Ac])
        E("act", lambda e: e.activation(out=B_[:, :TT], in_=A_[:, :TT], func=AF.Exp, scale=-0.5), reads=[Ac], writes=[Bc])
        E("dve", lambda e: e.scalar_tensor_tensor(out=Q_[:, :TT], in0=src[:, :TT], scalar=gqk[:, gcol:gcol + 1], in1=B_[:, :TT], op0=ALU.mult, op1=ALU.mult),
          reads=[src_c, c_gqk, Bc], writes=[Qc])
        release(p2c)
        p3, p3c = bank(hold=True)
        E("pe", lambda e: e.matmul(p3[:, :TT], perm_bf[:], Q_[:, :TT], start=True, stop=True), reads=[Qc, c_perm], writes=[p3c])
        cx.update(A_=A_, Ac=Ac, B_=B_, Bc=Bc, Q_=Q_, Qc=Qc, p3=p3, p3c=p3c)

    def qk_stage_c(cx):
        TT, dst, dst_cells = cx["TT"], cx["dst"], cx["dst_cells"]
        A_, Ac, B_, Bc, Q_, Qc, p3, p3c = (cx[k] for k in ("A_", "Ac", "B_", "Bc", "Q_", "Qc", "p3", "p3c"))
        E("dve", lambda e: e.tensor_tensor(out=A_[:, :TT], in0=Q_[:, :TT], in1=COS[:, :TT], op=ALU.mult), reads=[Qc, c_cos], writes=[Ac])
        E("dve", lambda e: e.tensor_tensor(out=B_[:, :TT], in0=p3[:, :TT], in1=SIN[:, :TT], op=ALU.mult), reads=[p3c, c_sin], writes=[Bc])
        E("dve", lambda e: e.tensor_tensor(out=dst, in0=A_[:, :TT], in1=B_[:, :TT], op=ALU.add), reads=[Ac, Bc], writes=dst_cells)
        release(p3c)
        release(cx["src_c"])

    def qk_norm_rope(src, src_c, gcol, TT, dst, dst_cells):
        cx = dict(src=src, src_c=src_c, gcol=gcol, TT=TT, dst=dst, dst_cells=dst_cells)
        qk_stage_a(cx)
        qk_stage_b(cx)
        qk_stage_c(cx)

    def proj_fm(w_, wc_, col0, TT, hold=False):
        pb, pbc = bank(hold=hold)
        for c in range(NCH):
            E("pe", lambda e, c=c: e.matmul(pb[:, :TT], w_[:, c, col0:col0 + 128], hT[:, c, :TT], start=(c == 0), stop=(c == NCH - 1)),
              reads=[wc_, hT_c[c]], writes=[pbc])
        return pb, pbc

    def kv_proj(TT, blk0):
        w_, wc_ = next_tile(("in", "A", 2, 0))
        nb = TT // 128
        for kc in range(2):
            pb, pbc = proj_fm(w_, wc_, kc * 128, TT)
            qk_norm_rope(pb, pbc, 1, TT, kT[:, kc, blk0 * 128: blk0 * 128 + TT], kT_c[blk0:blk0 + nb])
        for b in range(nb):
            pb, pbc = bank()
            for c in range(NCH):
                E("pe", lambda e, c=c, b=b, pb=pb: e.matmul(pb[:, 0:256], hT[:, c, b * 128:(b + 1) * 128], w_[:, c, 256:512], start=(c == 0), stop=(c == NCH - 1)),
                  reads=[wc_, hT_c[c]], writes=[pbc])
            E("act", lambda e, b=b, pb=pb: e.activation(out=Vaug[:, blk0 + b, :, 0:64], in_=pb[:, 0:256].rearrange("p (h d) -> p h d", h=4), func=AF.Copy),
              reads=[pbc], writes=[V_c[blk0 + b]])

    SQC = math.sqrt(0.044715)
    GC = 2.0 * math.sqrt(2.0 / math.pi)

    def gelu_from_psum(pb, pbc, n, dst, dst_cells):
        ts = nxt("tset", 2)
        A_, Ac = tA[ts], tA_c[ts]
        E("act", lambda e: e.activation(out=A_[:, :n], in_=pb[:, :n], func=AF.Square, scale=SQC), reads=[pbc], writes=[Ac])
        E("dve", lambda e: e.scalar_tensor_tensor(out=A_[:, :n], in0=A_[:, :n], scalar=1.0, in1=pb[:, :n], op0=ALU.add, op1=ALU.mult),
          reads=[Ac, pbc], writes=[Ac])
        E("act", lambda e: e.activation(out=A_[:, :n], in_=A_[:, :n], func=AF.Sigmoid, scale=GC), reads=[Ac], writes=[Ac])
        E("dve", lambda e: e.tensor_tensor(out=dst, in0=A_[:, :n], in1=pb[:, :n], op=ALU.mult), reads=[Ac, pbc], writes=dst_cells)

    def mixer2(first_block_of_core):
        TT = T
        rmsnorm_fm(16, TT)
        pend = []

        def push(cx):
            pend.append(cx)
            n = len(pend)
            if n >= 2:
                qk_stage_a(pend[n - 2])
            if n >= 3:
                qk_stage_b(pend[n - 3])
                qk_stage_c(pend[n - 3])

        wkv, wkvc = next_tile(("in", "A", 2, 0))
        for kc in range(2):
            pb, pbc = proj_fm(wkv, wkvc, kc * 128, TT, hold=True)
            push(dict(src=pb, src_c=pbc, gcol=1, TT=TT, dst=kT[:, kc, 128:128 + TT], dst_cells=kT_c[1:1 + NB]))
        for b in range(NB):
            pb, pbc = bank()
            for c in range(NCH):
                E("pe", lambda e, c=c, b=b, pb=pb: e.matmul(pb[:, 0:256], hT[:, c, b * 128:(b + 1) * 128], wkv[:, c, 256:512], start=(c == 0), stop=(c == NCH - 1)),
                  reads=[wkvc, hT_c[c]], writes=[pbc])
            E("act", lambda e, b=b, pb=pb: e.activation(out=Vaug[:, 1 + b, :, 0:64], in_=pb[:, 0:256].rearrange("p (h d) -> p h d", h=4), func=AF.Copy),
              reads=[pbc], writes=[V_c[1 + b]])
        for g in range(2):
            w_, wc_ = next_tile(("in", "A", g, 0))
            for m in range(4):
                qm = g * 4 + m
                pb, pbc = proj_fm(w_, wc_, m * 128, TT, hold=True)
                qa, qc = qT(qm)
                push(dict(src=pb, src_c=pbc, gcol=0, TT=TT, dst=qa, dst_cells=qc))
        n = len(pend)
        qk_stage_a(pend[n - 1])
        qk_stage_b(pend[n - 2]); qk_stage_c(pend[n - 2])
        qk_stage_b(pend[n - 1]); qk_stage_c(pend[n - 1])
        if first_block_of_core:
            dump("b", 0, qT(0)[0], qT(0)[1])
            dump("b", 512, kT[:, 0, :], kT_c)
            dump("b", 1152, Vaug[:].rearrange("p b h d -> p (b h d)"), V_c)
            dump("f", 1024, COS[:, :], [c_cos])
            dump("f", 1536, SIN[:, :], [c_sin])
        if mix_level < 2 and not DBG:
            return
        tiles_ = {}

        def get_tile(cg):
            if cg not in tiles_:
                tiles_[cg] = next_tile(("in", "A", cg, 0))
            return tiles_[cg]

        def gv_mm(gt, b):
            w_, wc_ = get_tile(5 + gt)
            pb, pbc = bank()
            for c in range(NCH):
                E("pe", lambda e, c=c, b=b, pb=pb, w_=w_: e.matmul(pb, hT[:, c, b * 128:(b + 1) * 128], w_[:, c, :], start=(c == 0), stop=(c == NCH - 1)),
                  reads=[wc_, hT_c[c]], writes=[pbc])
            return (pb, pbc)

        def gv_chain(gt, b, ctx):
            pb, pbc = ctx
            ts = nxt("tset", 2)
            B_, Bc = tB[ts], tB_c[ts]
            gelu_from_psum(pb, pbc, 512, B_[:, :512], [Bc])
            i2 = nxt("tset", 2)
            A2, A2c = tA[i2], tA_c[i2]
            E("act", lambda e, A2=A2, B_=B_: e.activation(out=A2[:, :512], in_=B_[:, :512], func=AF.Square), reads=[Bc], writes=[A2c])
            smi = nxt("small", 4)
            s4 = small[:, smi * 16: smi * 16 + 4]
            s4c = [small_c[smi * 2], small_c[smi * 2 + 1]]
            E("dve", lambda e, A2=A2, s4=s4: e.tensor_reduce(out=s4, in_=A2[:, :512].rearrange("p (g c) -> p g c", g=4), axis=AX.X, op=ALU.add),
              reads=[A2c], writes=s4c)
            E("act", lambda e, s4=s4: e.activation(out=s4, in_=s4, func=AF.Sqrt, bias=cst[:, 0:1], scale=1.0 / 128), reads=s4c + [c_cst], writes=s4c)
            E("dve", lambda e, s4=s4: e.reciprocal(out=s4, in_=s4), reads=s4c, writes=s4c)
            ga, gc_ = gvn(b)
            E("dve", lambda e, ga=ga, gt=gt, B_=B_, s4=s4: e.tensor_tensor(
                out=ga[:, gt * 512:(gt + 1) * 512].rearrange("p (g c) -> p g c", g=4), in0=B_[:, :512].rearrange("p (g c) -> p g c", g=4),
                in1=s4.unsqueeze(2).broadcast_to([128, 4, 128]), op=ALU.mult),
              reads=[Bc] + s4c, writes=gc_)

        def gu_mm(gt, m):
            w_, wc_ = get_tile(3 + gt)
            return proj_fm(w_, wc_, m * 128, TT)

        def gu_chain(gt, m, ctx):
            pb, pbc = ctx
            ua, uc = gu(gt * 4 + m)
            gelu_from_psum(pb, pbc, TT, ua, uc)

        groups = [(gv_mm, gv_chain, gt, b) for gt in range(2) for b in range(NB)] + [(gu_mm, gu_chain, gt, m) for gt in range(2) for m in range(4)]
        gi = [0]

        def group_mm():
            if gi[0] < len(groups):
                fm, fc, a0, a1 = groups[gi[0]]
                gi[0] += 1
                ctx = fm(a0, a1)
                return (fc, a0, a1, ctx)
            return None

        def group_chain(g):
            if g is not None:
                fc, a0, a1, ctx = g
                fc(a0, a1, ctx)

        def emit_scores(b, qp):
            qms = (2 * qp, 2 * qp + 1)
            kc = qp // 2
            sbanks = [bank(), bank()]
            for idx, qm in enumerate(qms):
                qa, qc = qT(qm)
                for hh in range(2):
                    off = hh * 64
                    sb_, sbc = sbanks[hh]
                    for kk in range(2):
                        kb = b + kk
                        E("pe", lambda e, off=off, kk=kk, kb=kb, idx=idx, qa=qa, sb_=sb_, kc=kc, b=b: e.matmul(
                            sb_[:, idx * 256 + kk * 128: idx * 256 + (kk + 1) * 128],
                            kT[off:off + 64, kc, kb * 128:(kb + 1) * 128], qa[off:off + 64, b * 128:(b + 1) * 128], start=True, stop=True),
                          reads=[kT_c[kb]] + qc, writes=[sbc])
            return sbanks

        def emit_exp_mask(b, sbanks):
            mk, mkc = (mask0_bf, c_mask0) if (first_block_of_core and b == 0) else (mask_bf, c_mask)
            pts = []
            for hh in range(2):
                sb_, sbc = sbanks[hh]
                pi = nxt("pt", 4)
                pt, ptc = ptb(pi)
                pts.append((pt, ptc))
                E("act", lambda e, sb_=sb_, pt=pt: e.activation(out=pt, in_=sb_, func=AF.Exp), reads=[sbc], writes=ptc)
                E("pool", lambda e, pt=pt, mk=mk: e.tensor_tensor(out=pt, in0=pt, in1=mk[:], op=ALU.mult), reads=ptc + [mkc], writes=ptc)
            return pts

        def emit_pv(b, qp, pts, abanks):
            for idx, qm in enumerate((2 * qp, 2 * qp + 1)):
                for hh in range(2):
                    pt, ptc = pts[hh]
                    h = 8 * (qm // 4) + (qm % 4) + 4 * hh
                    kvh = 2 * (qm // 4) + hh
                    ab, abc = abanks[h // 6]
                    col = (h % 6) * 65
                    for kk in range(2):
                        kb = b + kk
                        E("pe", lambda e, ab=ab, col=col, pt=pt, idx=idx, kk=kk, kb=kb, kvh=kvh: e.matmul(
                            ab[:, col:col + 65], pt[:, idx * 256 + kk * 128: idx * 256 + (kk + 1) * 128], Vaug[:, kb, kvh, :],
                            start=(kk == 0), stop=(kk == 1)),
                          reads=ptc + [V_c[kb]], writes=[abc])

        deferred_tr = []
        pts_carry = None
        for b in range(NB):
            ao, aoc = aout(b % 2)
            abanks = [bank(hold=True) for _ in range(3)]
            pts_cur = pts_carry if b > 0 else emit_exp_mask(b, emit_scores(b, 0))
            for qp in range(4):
                if qp < 3:
                    pts_next = emit_exp_mask(b, emit_scores(b, qp + 1))
                elif b + 1 < NB:
                    pts_next = emit_exp_mask(b + 1, emit_scores(b + 1, 0))
                    pts_carry = pts_next
                else:
                    pts_next = None
                g_ = group_mm()
                emit_pv(b, qp, pts_cur, abanks)
                group_chain(g_)
                pts_cur = pts_next
                if qp == 0 and deferred_tr:
                    deferred_tr.pop(0)()
            smi = nxt("small", 4)
            den = small[:, smi * 16: smi * 16 + 16]
            denc = [small_c[smi * 2], small_c[smi * 2 + 1]]
            for bi in range(3):
                ab, abc = abanks[bi]
                n = 6 if bi < 2 else 4
                E("dve", lambda e, ab=ab, n=n, bi=bi, den=den: e.tensor_tensor(
                    out=den[:, bi * 6: bi * 6 + n].unsqueeze(2), in0=ab[:, 0:n * 65].rearrange("p (h d) -> p h d", d=65)[:, :, 64:65],
                    in1=esink[:, bi * 6: bi * 6 + n].unsqueeze(2), op=ALU.add),
                  reads=[abc, c_esink], writes=denc)
            E("dve", lambda e, den=den: e.reciprocal(out=den, in_=den), reads=denc, writes=denc)
            for bi in range(3):
                ab, abc = abanks[bi]
                n = 6 if bi < 2 else 4
                E("dve", lambda e, ab=ab, n=n, bi=bi, den=den, ao=ao: e.tensor_tensor(
                    out=ao[:, bi * 384: bi * 384 + n * 64].rearrange("p (h d) -> p h d", d=64),
                    in0=ab[:, 0:n * 65].rearrange("p (h d) -> p h d", d=65)[:, :, 0:64],
                    in1=den[:, bi * 6: bi * 6 + n].unsqueeze(2).broadcast_to([128, n, 64]), op=ALU.mult),
                  reads=[abc] + denc, writes=aoc)
            for (_, abc_) in abanks:
                release(abc_)
            smj = nxt("small", 4)
            ss = small[:, smj * 16: smj * 16 + 1]
            ssc = [small_c[smj * 2], small_c[smj * 2 + 1]]
            E("dve", lambda e, ao=ao: e.tensor_tensor(out=junk, in0=ao, in1=ao, op=ALU.mult), reads=aoc, writes=[c_junk])
            E("dve", lambda e, ss=ss: e.tensor_reduce(out=ss, in_=junk.unsqueeze(1), axis=AX.X, op=ALU.add), reads=[c_junk], writes=ssc)
            E("act", lambda e, ss=ss: e.activation(out=ss, in_=ss, func=AF.Sqrt, bias=cst[:, 0:1], scale=1.0 / 1024), reads=ssc + [c_cst], writes=ssc)
            E("dve", lambda e, ss=ss: e.reciprocal(out=ss, in_=ss), reads=ssc, writes=ssc)
            E("dve", lambda e, ao=ao, ss=ss: e.tensor_scalar(out=ao, in0=ao, scalar1=ss, scalar2=None, op0=ALU.mult), reads=aoc + ssc, writes=aoc)
            def do_tr(b=b, ao=ao, aoc=aoc):
                for half in range(2):
                    tb, tbc = bank()
                    for k in range(4):
                        m = half * 4 + k
                        E("pe", lambda e, tb=tb, k=k, m=m, ao=ao: e.transpose(tb[:, k * 128:(k + 1) * 128], ao[:, m * 128:(m + 1) * 128], ident[:]),
                          reads=aoc + [c_ident], writes=[tbc])
                    E("dve", lambda e, tb=tb, half=half, b=b: e.tensor_tensor(
                        out=mixA[:, half * 4: half * 4 + 4, b * 128:(b + 1) * 128], in0=tb.rearrange("p (k t) -> p k t", k=4),
                        in1=g8[:, 8 + half * 4: 8 + half * 4 + 4].unsqueeze(2).broadcast_to([128, 4, 128]), op=ALU.mult),
                      reads=[tbc, c_g8], writes=mixA_c[half * 4: half * 4 + 4])
            deferred_tr.append(do_tr)
        while deferred_tr:
            deferred_tr.pop(0)()
        while gi[0] < len(groups):
            group_chain(group_mm())
        if mix_level < 4 and not DBG:
            return
        for g in range(8):
            pb, pbc = bank()
            for b in range(NB):
                ga, gc_ = gvn(b)
                E("pe", lambda e, pb=pb, b=b, g=g, ga=ga: e.matmul(pb[:, b * 128:(b + 1) * 128], ga[:, g * 128:(g + 1) * 128], wsT[:, g, :], start=True, stop=True),
                  reads=gc_ + [c_wsT], writes=[pbc])
            ts = nxt("tset", 2)
            A_, Ac = tA[ts], tA_c[ts]
            E("dve", lambda e, pb=pb, g=g, A_=A_: e.scalar_tensor_tensor(
                out=A_[:, :TT].rearrange("p (b t) -> p b t", b=NB), in0=pb.rearrange("p (b t) -> p b t", b=NB), scalar=g8[:, g:g + 1],
                in1=bias_bc[:, g, :].unsqueeze(1).broadcast_to([128, NB, 128]), op0=ALU.mult, op1=ALU.add),
              reads=[pbc, c_g8, c_bias], writes=[Ac])
            ua, uc = gu(g)
            E("dve", lambda e, ua=ua, A_=A_: e.tensor_tensor(out=ua, in0=ua, in1=A_[:, :TT], op=ALU.mult), reads=uc + [Ac], writes=uc)
        pb, pbc = bank()
        for g in range(8):
            ua, uc = gu(g)
            i = nxt("sq", 4)
            E("act", lambda e, ua=ua, i=i: e.activation(out=sq[i][:, :TT], in_=ua, func=AF.Square), reads=uc, writes=[sq_c[i]])
            E("pe", lambda e, i=i, g=g, pb=pb: e.matmul(pb[:, :TT], ones_bf[:], sq[i][:, :TT], start=(g == 0), stop=(g == 7)), reads=[sq_c[i], c_ones], writes=[pbc])
        rsqrt_tile(pb, pbc, 1.0 / 1024, TT, rstd, c_rstd)
        for g in range(8):
            ua, uc = gu(g)
            E("dve", lambda e, ua=ua, g=g: e.scalar_tensor_tensor(out=hT[:, 8 + g, :TT], in0=ua, scalar=g8[:, 16 + g:17 + g], in1=rstd[:, :TT], op0=ALU.mult, op1=ALU.mult),
              reads=uc + [c_g8, c_rstd], writes=[hT_c[8 + g]])
        for dg in range(4):
            w_, wc_ = next_tile(("out", "A", dg, 0))
            banks = [bank() for _ in range(4)]
            for c in range(NCH):
                for i in range(4):
                    pb, pbc = banks[i]
                    src_ = mixA[:, c, :TT] if c < 8 else hT[:, c, :TT]
                    srcc_ = mixA_c[c] if c < 8 else hT_c[c]
                    E("pe", lambda e, pb=pb, c=c, i=i, w_=w_, src_=src_: e.matmul(pb[:, :TT], w_[:, c, i * 128:(i + 1) * 128], src_, start=(c == 0), stop=(c == NCH - 1)),
                      reads=[wc_, srcc_], writes=[pbc])
            for i in range(4):
                pb, pbc = banks[i]
                c = dg * 4 + i
                E("dve", lambda e, pb=pb, c=c: e.tensor_tensor(out=xT[:, c, :TT], in0=pb[:, :TT], in1=xT[:, c, :TT], op=ALU.add),
                  reads=[pbc, xT_c[c]], writes=[xT_c[c]])
        E("dve", lambda e: e.tensor_copy(out=kT[:, :, 0:128], in_=kT[:, :, NB * 128:(NB + 1) * 128]), reads=[kT_c[NB]], writes=[kT_c[0]])
        E("dve", lambda e: e.tensor_copy(out=Vaug[:, 0, :, :], in_=Vaug[:, NB, :, :]), reads=[V_c[NB]], writes=[V_c[0]])

    def program():
        if do_mixer:
            load_x(0, HALO)
            rope_tables(0, HALO)
            ffn(0, "g1", "u1", "d1", HALO)
            rmsnorm_fm(16, HALO)
            kv_proj(HALO, 0)
        for p in range(npass):
            row0 = HALO + p * T
            load_x(row0, T)
            if do_mixer:
                rope_tables(row0, T)
            if do_ffn1:
                ffn(0, "g1", "u1", "d1", T)
            if do_mixer:
                mixer2(p == 0)
            if do_ffn2:
                ffn(32, "g2", "u2", "d2", T)
            store_x(p * T, T)

    dry[0] = True
    program()
    dry[0] = False
    bank_ctr[0] = 0
    held.clear()
    for k_ in rr:
        rr[k_] = 0
    del out_ops[:]
    for (key, kind, cg, pc) in stream:
        conv(key, cg, kind, pc)
    program()
    assert consumed[0] == len(stream)
    P.final_ops = out_ops + dbg_ops
    run_prog(nc, P, st)
    st.close()
    return nc


def host_constants():
    ident = np.eye(128, dtype=np.float32)
    bd = np.zeros((128, 128), np.float32)
    bd[:64, :64] = 1.0
    bd[64:, 64:] = 1.0
    perm = np.zeros((128, 128), np.float32)
    freq = np.zeros((128, 2), np.float32)
    inv_freq = (np.float32(500000.0) ** (-np.arange(0, 16, 2, dtype=np.float32) / np.float32(16))).astype(np.float32)
    for p in range(128):
        d = p % 64
        if d < 8:
            perm[p + 8, p] = 1.0
            freq[p, 0] = inv_freq[d]
            freq[p, 1] = -1.0
        elif d < 16:
            perm[p - 8, p] = 1.0
            freq[p, 0] = inv_freq[d - 8]
            freq[p, 1] = 1.0
        else:
            freq[p, 1] = 1.0
    j = np.arange(128)[:, None]
    i = np.arange(128)[None, :]
    mprev = (j > i).astype(np.float32)
    mcur = (j <= i).astype(np.float32)
    mask = np.concatenate([mprev, mcur, mprev, mcur], axis=1)
    mask0 = np.concatenate([np.zeros_like(mprev), mcur, np.zeros_like(mprev), mcur], axis=1)
    return dict(ident=ident, bd=bd, perm=perm, freq=freq, mask=mask, mask0=mask0)


def q_col_perm():
    cols = []
    for qm in range(8):
        ha = 8 * (qm // 4) + (qm % 4)
        for h in (ha, ha + 4):
            cols.extend(range(h * 64, (h + 1) * 64))
    return np.array(cols + list(range(1024, IN_COLS)), dtype=np.int64)


_NC_CACHE = {}


def run(inputs, npass=8, do_mixer=True, do_ffn2=True, do_ffn1=True, trace=False, mix_level=4):
    key = (npass, do_mixer, do_ffn2, do_ffn1, mix_level)
    if key not in _NC_CACHE:
        _NC_CACHE[key] = build_program(npass, do_mixer, do_ffn2, do_ffn1, mix_level)
    nc = _NC_CACHE[key]
    f = lambda a: np.ascontiguousarray(np.asarray(a, dtype=np.float32))
    x = np.asarray(inputs["x"], dtype=np.float32)
    pos = np.asarray(inputs["positions"], dtype=np.int32)
    cst = host_constants()
    w_in = f(np.asarray(inputs["w_in"])[0][:, q_col_perm()])
    shared = {
        "w_g1": f(inputs["ffn1_w_gate"][0]), "w_u1": f(inputs["ffn1_w_up"][0]), "w_d1": f(inputs["ffn1_w_down"][0]),
        "w_in": w_in, "w_out": f(inputs["w_out"][0]),
        "w_g2": f(inputs["ffn2_w_gate"][0]), "w_u2": f(inputs["ffn2_w_up"][0]), "w_d2": f(inputs["ffn2_w_down"][0]),
        "gains": f(np.concatenate([np.asarray(inputs[k])[0].reshape(16, 128).T for k in ("ffn1_norm", "mix_norm", "ffn2_norm")], axis=1)),
        "gqk": f(np.stack([np.tile(np.asarray(inputs["q_norm"])[0], 2), np.tile(np.asarray(inputs["k_norm"])[0], 2)], axis=1)),
        "g8": f(np.concatenate([np.asarray(inputs[k])[0].reshape(8, 128).T for k in ("gmlp_v_norm", "attn_out_norm", "gmlp_out_norm")], axis=1)),
        "sinks": f(np.asarray(inputs["attn_sinks"])[0].reshape(1, 16)),
        "wsT": f(np.asarray(inputs["gmlp_w_s"])[0].transpose(2, 0, 1).reshape(128, 1024)),
        "bs": f(np.asarray(inputs["gmlp_b_s"])[0].reshape(1, 1024)),
        "ident": cst["ident"], "bd": cst["bd"], "perm": cst["perm"], "mask": cst["mask"], "freq": cst["freq"],
    }
    in_maps = []
    for c in range(NCORES):
        b = c // 4
        s0 = (c % 4) * TOK_PER_CORE
        if s0 == 0:
            xc = np.concatenate([np.zeros((HALO, D), np.float32), x[b, 0:TOK_PER_CORE]], axis=0)
            pc = np.concatenate([np.zeros((HALO,), np.int32), pos[b, 0:TOK_PER_CORE]])
            m0 = cst["mask0"]
        else:
            xc = x[b, s0 - HALO: s0 + TOK_PER_CORE]
            pc = pos[b, s0 - HALO: s0 + TOK_PER_CORE]
            m0 = cst["mask"]
        d = dict(shared)
        d["x"] = np.ascontiguousarray(xc)
        d["pos"] = np.ascontiguousarray(pc.reshape(1, -1))
        d["mask0"] = m0
        in_maps.append(d)
    res = run_bass_kernel_spmd(nc, in_maps, core_ids=list(range(NCORES)), **({"trace": True} if trace else {}))
    out = np.empty((2, SEQ, D), np.float32)
    for c in range(NCORES):
        b = c // 4
        s0 = (c % 4) * TOK_PER_CORE
        out[b, s0:s0 + TOK_PER_CORE] = res.results[c]["y"]
    if mix_level >= 100:
        return out, res
    if trace:
        return out, res
    return out


def kernel(**inputs):
    return run(inputs)
```
